# Optimizing a Trainium2 kernel written in Bass

```python
import math
import jax, jax.numpy as jnp
from jax import lax
import numpy as np

D_MODEL = 2048
BATCH = 8
SEQ = 2048
DEPTH = 1
DEC_BATCH = 2
DEC_SEQ = 16384
PAST_LEN = 128

A_HEADS = 8
A_QK_DIM = 64
A_V_DIM = 2 * A_QK_DIM
A_QK_W = A_HEADS * 2 * A_QK_DIM
A_V_W = A_HEADS * A_V_DIM
B_GROUPS = ((128, 1), (512, 4), (2048, 16))
B_HEADS_PER_GROUP = 4
B_HEADS = B_HEADS_PER_GROUP * len(B_GROUPS)
B_HEAD_DIM = 128
B_W = B_HEADS * B_HEAD_DIM
B_OUT_W = B_HEADS_PER_GROUP * B_HEAD_DIM
BAND_BLOCK = max(w // (2 * r) for (w, r) in B_GROUPS)
D_FF = 4 * D_MODEL
ROPE_THETA = 500000.0
ROPE_FRACTION_DEN = 4
Q_BLOCK = 128
NORM_EPS = 1e-6
SUBLN_EPS = 1e-5
NEG_BIG = -1e30
IN_SPLITS = (A_QK_W, A_QK_W, A_V_W, B_W, B_W, B_W, D_MODEL, D_MODEL)
IN_WIDTH = sum(IN_SPLITS)

kernel_name = "hybrid_diff_dilated_gated_encoder"


def rmsnorm(x, g, eps):
    xf = x.astype(jnp.float32)
    y = xf * lax.rsqrt(jnp.mean(xf * xf, axis=-1, keepdims=True) + eps)
    return (y * g.astype(jnp.float32)).astype(x.dtype)


def partial_rope(x, rot_dim):
    S = x.shape[1]
    half = rot_dim // 2
    pos = jnp.arange(S, dtype=jnp.float32)
    inv = ROPE_THETA ** (-jnp.arange(0, rot_dim, 2, dtype=jnp.float32) / rot_dim)
    ang = pos[:, None] * inv[None, :]
    cos = jnp.cos(ang)[None, :, None, :].astype(x.dtype)
    sin = jnp.sin(ang)[None, :, None, :].astype(x.dtype)
    x1 = x[..., :half]
    x2 = x[..., half:rot_dim]
    return jnp.concatenate([x1 * cos - x2 * sin, x2 * cos + x1 * sin, x[..., rot_dim:]], axis=-1)


def diff_attention(q, k, v, lam, subln_g, lambda_init):
    Bsz, S, H, _, d = q.shape
    nq = S // Q_BLOCK
    q = q * (d ** -0.5)
    qb = q.reshape(Bsz, nq, Q_BLOCK, H, 2, d).transpose(1, 0, 2, 3, 4, 5)

    def block(qblk):
        s = jnp.einsum('bqhcd,bkhcd->bhcqk', qblk, k).astype(jnp.float32)
        p = jax.nn.softmax(s, axis=-1)
        a = p[:, :, 0] - lam * p[:, :, 1]
        return jnp.einsum('bhqk,bkhe->bqhe', a.astype(v.dtype), v)

    o = lax.map(block, qb)
    o = o.transpose(1, 0, 2, 3, 4).reshape(Bsz, S, H, 2 * d)
    o = rmsnorm(o, subln_g, SUBLN_EPS) * (1.0 - lambda_init)
    return o.reshape(Bsz, S, H * 2 * d)


def dilated_group(q, k, v, dilation, half):
    Bsz, S, Hg, dh = q.shape
    L = S // dilation
    N = Bsz * dilation

    def to_sub(t):
        return t.reshape(Bsz, L, dilation, Hg, dh).transpose(0, 2, 1, 3, 4).reshape(N, L, Hg, dh)

    qs, ks, vs = to_sub(q) * (dh ** -0.5), to_sub(k), to_sub(v)
    nb = -(-L // BAND_BLOCK)
    Lp = nb * BAND_BLOCK
    qs = jnp.pad(qs, ((0, 0), (0, Lp - L), (0, 0), (0, 0)))
    kv_pad = ((0, 0), (BAND_BLOCK, Lp - L + BAND_BLOCK), (0, 0), (0, 0))
    kp = jnp.pad(ks, kv_pad).reshape(N, nb + 2, BAND_BLOCK, Hg, dh)
    vp = jnp.pad(vs, kv_pad).reshape(N, nb + 2, BAND_BLOCK, Hg, dh)
    kw = jnp.concatenate([kp[:, :-2], kp[:, 1:-1], kp[:, 2:]], axis=2)
    vw = jnp.concatenate([vp[:, :-2], vp[:, 1:-1], vp[:, 2:]], axis=2)
    qb = qs.reshape(N, nb, BAND_BLOCK, Hg, dh)

    s = jnp.einsum('nbqhd,nbkhd->nbhqk', qb, kw).astype(jnp.float32)
    qpos = jnp.arange(nb)[:, None] * BAND_BLOCK + jnp.arange(BAND_BLOCK)[None, :]
    kpos = (jnp.arange(nb)[:, None] - 1) * BAND_BLOCK + jnp.arange(3 * BAND_BLOCK)[None, :]
    rel = kpos[:, None, :] - qpos[:, :, None]
    mask = (jnp.abs(rel) <= half) & (kpos[:, None, :] >= 0) & (kpos[:, None, :] < L)
    s = jnp.where(mask[None, :, None], s, NEG_BIG)
    m = jnp.max(s, axis=-1, keepdims=True)
    p = jnp.exp(s - m)
    l = jnp.sum(p, axis=-1, keepdims=True)
    o = jnp.einsum('nbhqk,nbkhd->nbqhd', (p / l).astype(v.dtype), vw)
    lse = (m + jnp.log(l))[..., 0].transpose(0, 1, 3, 2)

    o = o.reshape(N, Lp, Hg, dh)[:, :L]
    lse = lse.reshape(N, Lp, Hg)[:, :L]
    o = o.reshape(Bsz, dilation, L, Hg, dh).transpose(0, 2, 1, 3, 4).reshape(Bsz, S, Hg, dh)
    lse = lse.reshape(Bsz, dilation, L, Hg).transpose(0, 2, 1, 3).reshape(Bsz, S, Hg)
    return o, lse


def dilated_attention(q, k, v):
    Bsz, S = q.shape[:2]
    outs, lses = [], []
    for g, (window, dilation) in enumerate(B_GROUPS):
        sl = slice(g * B_HEADS_PER_GROUP, (g + 1) * B_HEADS_PER_GROUP)
        o, lse = dilated_group(q[:, :, sl], k[:, :, sl], v[:, :, sl], dilation, window // (2 * dilation))
        outs.append(o)
        lses.append(lse)
    w = jax.nn.softmax(jnp.stack(lses, axis=0), axis=0)
    o = jnp.sum(w[..., None] * jnp.stack(outs, axis=0).astype(jnp.float32), axis=0)
    return o.astype(q.dtype).reshape(Bsz, S, B_OUT_W)


def trunk(x, norm_mix, w_in, lambda_q1, lambda_k1, lambda_q2, lambda_k2, subln_g,
          w_proj_a, w_proj_b, w_out, norm_ffn, w1, w2, norm_final):
    Bsz, S, _ = x.shape
    split_idx = [int(i) for i in np.cumsum(IN_SPLITS)[:-1]]
    for l in range(DEPTH):
        lambda_init = 0.8 - 0.6 * math.exp(-0.3 * l)
        h = rmsnorm(x, norm_mix[l], NORM_EPS)
        z = h @ w_in[l]
        qa, ka, va, qb, kb, vb, ga, gb = jnp.split(z, split_idx, axis=-1)
        qa = partial_rope(qa.reshape(Bsz, S, 2 * A_HEADS, A_QK_DIM), A_QK_DIM // ROPE_FRACTION_DEN)
        ka = partial_rope(ka.reshape(Bsz, S, 2 * A_HEADS, A_QK_DIM), A_QK_DIM // ROPE_FRACTION_DEN)
        qa = qa.reshape(Bsz, S, A_HEADS, 2, A_QK_DIM)
        ka = ka.reshape(Bsz, S, A_HEADS, 2, A_QK_DIM)
        va = va.reshape(Bsz, S, A_HEADS, A_V_DIM)
        lam = (jnp.exp(jnp.sum(lambda_q1[l].astype(jnp.float32) * lambda_k1[l].astype(jnp.float32)))
               - jnp.exp(jnp.sum(lambda_q2[l].astype(jnp.float32) * lambda_k2[l].astype(jnp.float32)))
               + lambda_init)
        ya = diff_attention(qa, ka, va, lam, subln_g[l], lambda_init) @ w_proj_a[l]
        qb = partial_rope(qb.reshape(Bsz, S, B_HEADS, B_HEAD_DIM), B_HEAD_DIM // ROPE_FRACTION_DEN)
        kb = partial_rope(kb.reshape(Bsz, S, B_HEADS, B_HEAD_DIM), B_HEAD_DIM // ROPE_FRACTION_DEN)
        vb = vb.reshape(Bsz, S, B_HEADS, B_HEAD_DIM)
        yb = dilated_attention(qb, kb, vb) @ w_proj_b[l]
        merged = jax.nn.sigmoid(ga) * ya + jax.nn.sigmoid(gb) * yb
        x = x + merged @ w_out[l]
        h = rmsnorm(x, norm_ffn[l], NORM_EPS)
        x = x + jnp.square(jax.nn.relu(h @ w1[l])) @ w2[l]
    return rmsnorm(x, norm_final, NORM_EPS)


def setup_inputs(seed: int = 0) -> dict:
    key = jax.random.key(seed)
    ks = jax.random.split(key, 17)
    f32 = jnp.float32

    def nrm(k, shape, scale):
        return jax.random.normal(k, shape, f32) * scale

    return {
        "x_prompt": nrm(ks[0], (BATCH, SEQ, D_MODEL), 1.0),
        "x_sample": nrm(ks[1], (DEC_BATCH, DEC_SEQ, D_MODEL), 1.0),
        "norm_mix": 1.0 + nrm(ks[2], (DEPTH, D_MODEL), 0.02),
        "w_in": nrm(ks[3], (DEPTH, D_MODEL, IN_WIDTH), D_MODEL ** -0.5),
        "lambda_q1": nrm(ks[4], (DEPTH, A_QK_DIM), 0.1),
        "lambda_k1": nrm(ks[5], (DEPTH, A_QK_DIM), 0.1),
        "lambda_q2": nrm(ks[6], (DEPTH, A_QK_DIM), 0.1),
        "lambda_k2": nrm(ks[7], (DEPTH, A_QK_DIM), 0.1),
        "subln_g": 1.0 + nrm(ks[8], (DEPTH, A_V_DIM), 0.02),
        "w_proj_a": nrm(ks[9], (DEPTH, A_V_W, D_MODEL), A_V_W ** -0.5),
        "w_proj_b": nrm(ks[10], (DEPTH, B_OUT_W, D_MODEL), B_OUT_W ** -0.5),
        "w_out": nrm(ks[11], (DEPTH, D_MODEL, D_MODEL), D_MODEL ** -0.5),
        "norm_ffn": 1.0 + nrm(ks[12], (DEPTH, D_MODEL), 0.02),
        "w1": nrm(ks[13], (DEPTH, D_MODEL, D_FF), D_MODEL ** -0.5),
        "w2": nrm(ks[14], (DEPTH, D_FF, D_MODEL), D_FF ** -0.5),
        "norm_final": 1.0 + nrm(ks[15], (D_MODEL,), 0.02),
    }


def reference(x_prompt, x_sample, norm_mix, w_in, lambda_q1, lambda_k1, lambda_q2, lambda_k2,
              subln_g, w_proj_a, w_proj_b, w_out, norm_ffn, w1, w2, norm_final):
    y_prompt = trunk(x_prompt, norm_mix, w_in, lambda_q1, lambda_k1, lambda_q2, lambda_k2, subln_g,
                     w_proj_a, w_proj_b, w_out, norm_ffn, w1, w2, norm_final)
    y_sample = trunk(x_sample, norm_mix, w_in, lambda_q1, lambda_k1, lambda_q2, lambda_k2, subln_g,
                     w_proj_a, w_proj_b, w_out, norm_ffn, w1, w2, norm_final)
    return (y_prompt, y_sample)
```

```python
import numpy as np
import concourse.bass as bass
import concourse.mybir as mybir

F32 = mybir.dt.float32
BF16 = mybir.dt.bfloat16
AF = mybir.ActivationFunctionType
ALU = mybir.AluOpType
AX = mybir.AxisListType

ENGS = ("pe", "act", "dve", "pool", "sp")
DMA_RING = {"sp": 8, "pool": 6, "act": 4}


class Op:
    __slots__ = ("eng", "fn", "deps", "dma", "sig", "sigval", "dsem", "dval", "idx", "qidx", "cdeps")

    def __init__(self, eng, fn, dma):
        self.eng = eng
        self.fn = fn
        self.dma = dma
        self.deps = set()
        self.sig = False
        self.sigval = 0
        self.dsem = None
        self.dval = 0
        self.qidx = -1


class Sched:
    def __init__(self):
        self.ops = []
        self.last_w = {}
        self.readers = {}
        self.const = set()
        self.last_on = {e: None for e in ENGS}
        self.bar_deps = {e: set() for e in ENGS}
        self.dma_q = {q: [] for q in DMA_RING}

    def add(self, eng, fn, reads=(), writes=(), dma=False):
        op = Op(eng, fn, dma)
        op.idx = len(self.ops)
        deps = op.deps
        for b in reads:
            w = self.last_w.get(b)
            if w is not None:
                deps.add(w)
        for b in writes:
            w = self.last_w.get(b)
            if w is not None:
                deps.add(w)
            for r in self.readers.get(b, ()):
                deps.add(r)
        for b in reads:
            if b not in self.const:
                self.readers.setdefault(b, []).append(op.idx)
        for b in writes:
            self.last_w[b] = op.idx
            self.readers[b] = []
        if self.bar_deps[eng]:
            deps |= self.bar_deps[eng]
            self.bar_deps[eng] = set()
        deps.discard(op.idx)
        if dma:
            op.qidx = len(self.dma_q[eng])
            self.dma_q[eng].append(op.idx)
        self.ops.append(op)
        self.last_on[eng] = op.idx
        return op.idx

    def freeze(self, *keys):
        for k in keys:
            self.const.add(k)

    def barrier(self):
        allp = set()
        for e in ENGS:
            if self.last_on[e] is not None:
                allp.add(self.last_on[e])
        for q, lst in self.dma_q.items():
            for i in lst[-DMA_RING[q]:]:
                allp.add(i)
        for e in ENGS:
            self.bar_deps[e] |= allp
        self.last_w = {k: v for k, v in self.last_w.items() if k in self.const}
        self.readers = {}

    def emit(self, nc, es):
        ops = self.ops
        for op in ops:
            best = {}
            for d in op.deps:
                p = ops[d]
                if p.dma:
                    continue
                if p.eng == "pe" and op.eng == "pe" and not op.dma:
                    continue
                if p.eng not in best or best[p.eng] < d:
                    best[p.eng] = d
            op.cdeps = list(best.values())
            for d in op.cdeps:
                ops[d].sig = True
        cnt = {e: 0 for e in ENGS}
        for op in ops:
            if op.dma:
                continue
            if op.sig:
                cnt[op.eng] += 1
                op.sigval = cnt[op.eng]
        csem = {e: es.enter_context(nc.semaphore("c_" + e)) for e in ("pe", "act", "dve", "pool")}
        dsem = {q: [es.enter_context(nc.semaphore("d_%s%d" % (q, i))) for i in range(n)]
                for q, n in DMA_RING.items()}
        for q, lst in self.dma_q.items():
            n = DMA_RING[q]
            for j, i in enumerate(lst):
                ops[i].dsem = dsem[q][j % n]
                ops[i].dval = 16 * (j // n + 1)
        per_eng = {e: [op for op in ops if op.eng == e] for e in ENGS}
        stats = {e: [len(per_eng[e]), 0] for e in ENGS}

        def run(ename, eng):
            known = {}

            def wait(sem, val):
                k = id(sem)
                if known.get(k, 0) >= val:
                    return
                known[k] = val
                eng.wait_ge(sem, val)
                stats[ename][1] += 1

            last = None
            for op in per_eng[ename]:
                dm = {}
                for d in op.deps:
                    p = ops[d]
                    if p.dma:
                        k = id(p.dsem)
                        if k not in dm or dm[k][1] < p.dval:
                            dm[k] = (p.dsem, p.dval)
                for sem_, val_ in dm.values():
                    wait(sem_, val_)
                for d in op.cdeps:
                    p = ops[d]
                    wait(csem[p.eng], p.sigval)
                if op.dma:
                    if op.dval > 16:
                        wait(op.dsem, op.dval - 16)
                    ins = op.fn(eng)
                    ins.then_inc(op.dsem, 16)
                    last = op
                else:
                    ins = op.fn(eng)
                    if op.sig:
                        ins.then_inc(csem[ename], 1)
            if ename in self.dma_q:
                for i in self.dma_q[ename][-DMA_RING[ename]:]:
                    wait(ops[i].dsem, ops[i].dval)

        with nc.Block() as block:
            @block.tensor
            def _(e):
                run("pe", e)

            @block.scalar
            def _(e):
                run("act", e)

            @block.vector
            def _(e):
                run("dve", e)

            @block.gpsimd
            def _(e):
                run("pool", e)

            @block.sync
            def _(e):
                run("sp", e)
        return stats


class Arena:
    def __init__(self, raw, nbytes):
        self.raw = raw
        self.nbytes = nbytes
        self.off = 0
        self.marks = []

    def alloc(self, shape_free, dtype):
        esz = 2 if dtype == BF16 else 4
        n = int(np.prod(shape_free))
        nb = n * esz
        off = (self.off + 63) // 64 * 64
        assert off + nb <= self.nbytes, ("SBUF arena overflow", off, nb, self.nbytes)
        self.off = off + nb
        ap = self.raw[:, off:off + nb].bitcast(dtype)
        if len(shape_free) > 1:
            names = " ".join("d%d" % i for i in range(len(shape_free)))
            kw = {"d%d" % i: int(s) for i, s in enumerate(shape_free)}
            ap = ap.rearrange("p (%s) -> p %s" % (names, names), **kw)
        return ap

    def push(self):
        self.marks.append(self.off)

    def pop(self):
        self.off = self.marks.pop()

from contextlib import ExitStack
from concourse.bass_utils import run_bass_kernel_spmd

D = 2048
DFF = 8192
INW = 11776
U8 = mybir.dt.uint8
ARENA_BYTES = 186 * 1024
CG = {"qA": [0, 1], "kA": [2, 3], "vA": [4, 5], "qB": [6, 7, 8], "kB": [9, 10, 11],
      "vB": [12, 13, 14], "gA": [15, 16, 17, 18], "gB": [19, 20, 21, 22]}
NEED = {"own": {"qA", "kA", "vA", "qB", "kB", "vB", "gA", "gB"},
        "halo": {"kA", "vA", "kB", "vB"}, "pad": set(), "far": {"kA", "vA"}}
OWN0 = 1024
B_GROUPS_R = (1, 4, 16)


class Job:
    pass


def build_program(NSS, jobs_enabled=("p", "s"), phases=(1, 2, 3, 4), p0=True):
    nc = bass.Bass("TRN2", target_bir_lowering=False)

    def din(name, shape, dt=F32):
        return nc.dram_tensor(name, list(shape), dt, kind="ExternalInput").ap()

    def dout(name, shape, dt=F32):
        return nc.dram_tensor(name, list(shape), dt, kind="ExternalOutput").ap()

    def dscr(name, shape, dt=BF16):
        return nc.dram_tensor(name, list(shape), dt).ap()

    w_in = din("w_in", [D, INW])
    w_pa = din("w_pa", [1024, D])
    w_pb = din("w_pb", [512, D])
    w_out = din("w_out", [D, D])
    w_1 = din("w_1", [D, DFF])
    w_2 = din("w_2", [DFF, D])
    g_mix = din("g_mix", [128, 16])
    g_ffn = din("g_ffn", [128, 16])
    g_fin = din("g_fin", [1, D])
    lam_in = din("lam_in", [1, 256])
    subln = din("subln", [128, 1])

    jobs = []
    for nm in jobs_enabled:
        J = Job()
        J.name = nm
        if nm == "p":
            J.NS, J.T = 4096, 2048
            J.akeys = (1024, 3072)
            J.cls = ["pad"] * 8 + ["own"] * 16 + ["pad"] * 8
        else:
            J.NS, J.T = NSS, 4096
            J.akeys = (0, NSS)
            J.cls = ["halo"] * 8 + ["own"] * 32 + ["halo"] * 8 + ["far"] * ((NSS - 6144) // 128)
        J.EXT = J.T + 2048
        J.nchunk = J.T // 2048
        J.x = din("x_" + nm, [J.NS, D])
        J.rope = din("rope_" + nm, [J.NS, 256])
        J.vmask = din("vmask_" + nm, [128, J.nchunk * 3 * 32])
        J.y = dout("y_" + nm, [J.T, D])
        J.kAT = dscr("kAT_" + nm, [8, 128, J.NS])
        J.vA = dscr("vA_" + nm, [J.NS, 1024])
        J.qAT = dscr("qAT_" + nm, [8, 128, J.T])
        J.kBT = dscr("kBT_" + nm, [12, 128, J.EXT])
        J.qBT = dscr("qBT_" + nm, [12, 128, J.T])
        J.vB = dscr("vB_" + nm, [J.EXT, 1536])
        J.gT = dscr("gT_" + nm, [32, 128, J.T])
        J.oAT = dscr("oAT_" + nm, [8, 128, J.T])
        J.oBT = dscr("oBT_" + nm, [4, 128, J.T])
        jobs.append(J)

    wb_in = dscr("wb_in", [23, 128, 16, 512])
    wb_pa = dscr("wb_pa", [128, 8, D])
    wb_pb = dscr("wb_pb", [128, 4, D])
    wb_out = dscr("wb_out", [4, 128, 16, 512])
    wb_1 = dscr("wb_1", [16, 128, 16, 512])
    wb_2 = dscr("wb_2", [4, 4, 128, 16, 512])

    S = Sched()
    es = ExitStack()
    with es:
        raw = es.enter_context(nc.sbuf_tensor("arena", [128, ARENA_BYTES], U8))
        ps = es.enter_context(nc.psum_tensor("ps", [128, 4096], F32))
        AR = Arena(raw, ARENA_BYTES)

        def bank(i, n=512):
            return ps[:, i * 512:i * 512 + n]

        def MM(out, lhsT, rhs, start=True, stop=True, r=(), w=()):
            S.add("pe", lambda e: e.matmul(out, lhsT, rhs, start=start, stop=stop), r, w)

        def TR(out, in_, r=(), w=()):
            S.add("pe", lambda e: e.transpose(out, in_, ident), list(r), w)

        def ACT(out, in_, func, r=(), w=(), scale=1.0, bias=0.0, accum=None):
            if accum is None:
                S.add("act", lambda e: e.activation(out=out, in_=in_, func=func, bias=bias, scale=scale), r, w)
            else:
                S.add("act", lambda e: e.activation(out=out, in_=in_, func=func, bias=bias, scale=scale,
                                                    accum_out=accum), r, w)

        def DMA(q, out, in_, r=(), w=()):
            S.add(q, lambda e: e.dma_start(out=out, in_=in_), r, w, dma=True)

        def TT(eng, out, a, b, op, r=(), w=()):
            S.add(eng, lambda e: e.tensor_tensor(out, a, b, op), r, w)

        def TSC(eng, out, a, s1, s2, op0, op1=None, r=(), w=()):
            if op1 is None:
                S.add(eng, lambda e: e.tensor_scalar(out, a, s1, None, op0), r, w)
            else:
                S.add(eng, lambda e: e.tensor_scalar(out, a, s1, s2, op0, op1), r, w)

        def STT(eng, out, in0, scalar, in1, op0, op1, r=(), w=()):
            S.add(eng, lambda e: e.scalar_tensor_tensor(out, in0, scalar, in1, op0, op1), r, w)

        def CP(eng, out, in_, r=(), w=()):
            if eng == "act":
                S.add("act", lambda e: e.activation(out=out, in_=in_, func=AF.Copy), r, w)
            else:
                S.add(eng, lambda e: e.tensor_copy(out, in_), r, w)

        def bcast_mid(ap2d, reps):
            pat = [list(ap2d.ap[0])] + [[0, int(rp)] for rp in reps] + [list(ap2d.ap[-1])]
            return bass.AP(ap2d.tensor, ap2d.offset, pat)

        def bcast_inner(ap2d, n):
            pat = [list(ap2d.ap[0]), list(ap2d.ap[1]), [0, int(n)]]
            return bass.AP(ap2d.tensor, ap2d.offset, pat)

        ident = AR.alloc([128], BF16)
        ones = AR.alloc([128], BF16)
        maskA = AR.alloc([128], BF16)
        maskB = AR.alloc([128], BF16)
        iot = AR.alloc([128], F32)
        neglam = AR.alloc([1], F32)
        gcol = AR.alloc([1], F32)
        lamt = AR.alloc([256], F32)
        lamp = AR.alloc([128], F32)
        lams = AR.alloc([2], F32)
        S.add("pool", lambda e: e.iota(iot, [[1, 128]], channel_multiplier=-1, allow_small_or_imprecise_dtypes=True),
              (), ["iot"])
        TSC("dve", ident, iot, 0.0, None, ALU.is_equal, r=["iot"], w=["ident"])
        TSC("dve", maskA, iot, 0.0, None, ALU.is_le, r=["iot"], w=["maskA"])
        TSC("dve", maskB, iot, 0.0, None, ALU.is_ge, r=["iot"], w=["maskB"])
        S.add("pool", lambda e: e.memset(ones, 1.0), (), ["ones"])
        DMA("sp", lamt, lam_in.partition_broadcast(128)[:, 0, :], w=["lamt"])
        DMA("sp", gcol, subln, w=["gcol0"])
        TT("dve", lamp[:, 0:64], lamt[:, 0:64], lamt[:, 64:128], ALU.mult, r=["lamt"], w=["lamp"])
        TT("dve", lamp[:, 64:128], lamt[:, 128:192], lamt[:, 192:256], ALU.mult, r=["lamt"], w=["lamp"])
        S.add("dve", lambda e: e.reduce_sum(lams, lamp.rearrange("p (a b) -> p a b", a=2), AX.X), ["lamp"], ["lams"])
        ACT(lams, lams, AF.Exp, r=["lams"], w=["lams"])
        TT("dve", neglam, lams[:, 1:2], lams[:, 0:1], ALU.subtract, r=["lams"], w=["neglam"])
        TSC("dve", neglam, neglam, -0.2, None, ALU.add, r=["neglam"], w=["neglam"])
        TSC("dve", gcol, gcol, 0.8, None, ALU.mult, r=["gcol0"], w=["gcol"])
        S.freeze("ident", "ones", "maskA", "maskB", "neglam", "gcol")

        def phase0():
            AR.push()
            stf = [AR.alloc([4, 512], F32) for _ in range(4)]
            stb = [AR.alloc([4, 512], BF16) for _ in range(4)]
            cnt = [0]

            def cast(dst, src, key, nk=4):
                i = cnt[0] % 4
                ceng = "dve" if cnt[0] % 2 == 0 else "act"
                cnt[0] += 1
                DMA("sp", stf[i][:, 0:nk, :], src, w=[("stf", i)])
                CP(ceng, stb[i][:, 0:nk, :], stf[i][:, 0:nk, :], r=[("stf", i)], w=[("stb", i)])
                DMA("pool", dst, stb[i][:, 0:nk, :], r=[("stb", i)], w=[key])

            def rows(wap, r0, c0):
                return wap[r0:r0 + 512, c0:c0 + 512].rearrange("(k p) n -> p k n", p=128)

            for cg in ([2, 3, 4, 5, 9, 10, 11, 12, 13, 14, 0, 1, 6, 7, 8] + list(range(15, 23))):
                for kq in range(4):
                    cast(wb_in[cg][:, kq * 4:(kq + 1) * 4, :], rows(w_in, kq * 512, cg * 512), ("wb_in", cg))
            for fg in range(4):
                for hq in range(2):
                    cast(wb_pa[:, hq * 4:(hq + 1) * 4, fg * 512:(fg + 1) * 512],
                         w_pa[hq * 512:(hq + 1) * 512, fg * 512:(fg + 1) * 512].rearrange("(h e) f -> e h f", e=128), "wb_pa")
                cast(wb_pb[:, :, fg * 512:(fg + 1) * 512],
                     w_pb[:, fg * 512:(fg + 1) * 512].rearrange("(h e) f -> e h f", e=128), "wb_pb")
            for og in range(4):
                for kq in range(4):
                    cast(wb_out[og][:, kq * 4:(kq + 1) * 4, :], rows(w_out, kq * 512, og * 512), ("wb_out", og))
            for fb in range(16):
                for kq in range(4):
                    cast(wb_1[fb][:, kq * 4:(kq + 1) * 4, :], rows(w_1, kq * 512, fb * 512), ("wb_1", fb))
            for og in range(4):
                for qd in range(4):
                    for kq in range(4):
                        cast(wb_2[og, qd][:, kq * 4:(kq + 1) * 4, :], rows(w_2, qd * 2048 + kq * 512, og * 512),
                             ("wb_2", og, qd))
            AR.pop()
            S.barrier()

        if p0:
            phase0()

        def phase1(J):
            AR.push()
            gmc = AR.alloc([16], F32)
            hT = AR.alloc([16, 2048], BF16)
            xt = [AR.alloc([D], F32) for _ in range(2)]
            hb = [AR.alloc([D], BF16) for _ in range(2)]
            sqj = AR.alloc([D], BF16)
            ssum = [AR.alloc([1], F32) for _ in range(2)]
            rp = AR.alloc([16, 256], F32)
            tmx = AR.alloc([128], F32)
            wsl = [AR.alloc([16, 512], BF16) for _ in range(2)]
            zb = [AR.alloc([512], BF16) for _ in range(2)]
            tmc = AR.alloc([128], F32)
            tms = AR.alloc([128], F32)
            ostg = [AR.alloc([8192], BF16) for _ in range(2)]
            DMA("sp", gmc, g_mix, w=["gmc"])
            if "pad" in J.cls:
                S.add("pool", lambda e: e.memset(ostg[0], 0.0), (), [("ostg", 0)])
                zt = ostg[0]
                for s0_, s1_ in ((0, 1024), (J.NS - 1024, J.NS)):
                    for hd in range(12):
                        DMA("sp", J.kBT[hd, :, s0_:s1_], zt[:, 0:1024], r=[("ostg", 0)], w=[("kBT", J.name)])
                    for t4 in range(2):
                        DMA("sp", J.vB[s0_ + t4 * 512:s0_ + (t4 + 1) * 512, :].rearrange("(t p) n -> p t n", p=128),
                            zt[:, 0:6144].rearrange("p (t n) -> p t n", t=4), r=[("ostg", 0)], w=[("vB", J.name)])
            nst = J.NS // 2048
            wcnt = [0]
            ocnt = [0]
            pcnt = [0]
            for st in range(nst):
                s0 = st * 2048
                cls = J.cls[st * 16:(st + 1) * 16]
                DMA("sp", rp, J.rope[s0:s0 + 2048, :].rearrange("(t p) c -> p t c", p=128), w=["rp"])
                for t in range(16):
                    if cls[t] == "pad":
                        continue
                    b = t % 2
                    DMA("sp", xt[b], J.x[s0 + t * 128:s0 + (t + 1) * 128, :], w=[("xt", b)])
                    ACT(sqj, xt[b], AF.Square, r=[("xt", b)], w=["sqj", ("ss", b)], accum=ssum[b])
                    ACT(ssum[b], ssum[b], AF.Ln, r=[("ss", b)], w=[("ss", b)], scale=1.0 / D, bias=1e-6)
                    ACT(ssum[b], ssum[b], AF.Exp, r=[("ss", b)], w=[("ss", b)], scale=-0.5)
                    TSC("pool", hb[b], xt[b], ssum[b], None, ALU.mult, r=[("xt", b), ("ss", b)], w=[("hb", b)])
                    for half in range(2):
                        pb = bank(half).bitcast(BF16)
                        for c in range(8):
                            cc = half * 8 + c
                            TR(pb[:, c * 128:(c + 1) * 128], hb[b][:, cc * 128:(cc + 1) * 128],
                               r=[("hb", b), "ident"], w=[("psT", half)])
                        TT("dve", hT[:, half * 8:(half + 1) * 8, t * 128:(t + 1) * 128],
                           pb.rearrange("p (c n) -> p c n", c=8), bcast_inner(gmc[:, half * 8:(half + 1) * 8], 128),
                           ALU.mult, r=[("psT", half), "gmc"], w=[("hT", t)])
                hTr = [("hT", t) for t in range(16)]
                import os as _os
                _kinds = _os.environ.get("DBG_KINDS", "kA,vA,kB,vB,qA,qB,gA,gB").split(",")
                for kind in ("kA", "vA", "kB", "vB", "qA", "qB", "gA", "gB"):
                    if kind not in _kinds:
                        continue
                    tiles = [t for t in range(16) if kind in NEED[cls[t]]]
                    if not tiles:
                        continue
                    t_lo, t_hi = tiles[0], tiles[-1] + 1
                    assert tiles == list(range(t_lo, t_hi))
                    ntile = t_hi - t_lo
                    for gi, cg in enumerate(CG[kind]):
                        wbuf = wcnt[0] % 2
                        wcnt[0] += 1
                        W = wsl[wbuf]
                        DMA("sp", W, wb_in[cg], r=[("wb_in", cg)], w=[("wsl", wbuf)])
                        ob = ocnt[0] % 2
                        ocnt[0] += 1
                        OS = ostg[ob]
                        if kind in ("qA", "kA", "qB", "kB"):
                            isA = kind in ("qA", "kA")
                            nm_, hf = (8, 8) if isA else (4, 16)
                            co = 0 if isA else 128
                            OSv = OS.rearrange("p (c n) -> p c n", c=4)
                            pend = [None]
                            for t in tiles:
                                pbk = pcnt[0] % 2
                                pcnt[0] += 1
                                P = bank(2 + pbk)
                                for k in range(16):
                                    MM(P, hT[:, k, t * 128:(t + 1) * 128], W[:, k, :], start=(k == 0), stop=(k == 15),
                                       r=[("hT", t), ("wsl", wbuf)], w=[("psP", pbk)])
                                Z = zb[pbk]
                                CP("act", Z, P, r=[("psP", pbk)], w=[("zb", pbk)])
                                Pm_ = P.rearrange("p (m d) -> p m d", m=nm_)
                                nh = nm_ * hf
                                tx3 = tmx.rearrange("p (m d) -> p m d", m=nm_)
                                CP("act", tx3, Pm_[:, :, 0:2 * hf], r=[("psP", pbk)], w=["tmx"])
                                x1 = tx3[:, :, 0:hf]
                                x2 = tx3[:, :, hf:2 * hf]
                                cos3 = rp[:, t, co:co + nh].rearrange("p (m d) -> p m d", m=nm_)
                                sin3 = rp[:, t, co + nh:co + 2 * nh].rearrange("p (m d) -> p m d", m=nm_)
                                t1 = tmc[:, 0:nh].rearrange("p (m d) -> p m d", m=nm_)
                                t3 = tmc[:, nh:2 * nh].rearrange("p (m d) -> p m d", m=nm_)
                                t2 = tms[:, 0:nh].rearrange("p (m d) -> p m d", m=nm_)
                                t4 = tms[:, nh:2 * nh].rearrange("p (m d) -> p m d", m=nm_)
                                TT("dve", t1, x1, cos3, ALU.mult, r=["tmx", "rp"], w=["tmc"])
                                TT("dve", t3, x2, cos3, ALU.mult, r=["tmx", "rp"], w=["tmc"])
                                TT("dve", t2, x2, sin3, ALU.mult, r=["tmx", "rp"], w=["tms"])
                                TT("dve", t4, x1, sin3, ALU.mult, r=["tmx", "rp"], w=["tms"])
                                Zv = Z.rearrange("p (m d) -> p m d", m=nm_)
                                TT("dve", Zv[:, :, 0:hf], t1, t2, ALU.subtract, r=["tmc", "tms"], w=[("zb", pbk)])
                                TT("dve", Zv[:, :, hf:2 * hf], t3, t4, ALU.add, r=["tmc", "tms"], w=[("zb", pbk)])
                                def _tr(t=t, pbk=pbk, Z=Z, OSv=OSv, ob=ob, t_lo=t_lo):
                                    ptb = bank(4 + pbk).bitcast(BF16)[:, 0:512]
                                    for c in range(4):
                                        TR(ptb[:, c * 128:(c + 1) * 128], Z[:, c * 128:(c + 1) * 128],
                                           r=[("zb", pbk), "ident"], w=[("psZ", pbk)])
                                    CP("dve", OSv[:, :, (t - t_lo) * 128:(t - t_lo + 1) * 128],
                                       ptb.rearrange("p (c n) -> p c n", c=4), r=[("psZ", pbk)], w=[("ostg", ob)])
                                if pend[0] is not None:
                                    pend[0]()
                                pend[0] = _tr
                            if pend[0] is not None:
                                pend[0]()
                                pend[0] = None
                            for c in range(4):
                                hd = gi * 4 + c
                                src = OSv[:, c, 0:ntile * 128]
                                sl0 = s0 + t_lo * 128
                                if kind == "kA":
                                    dst = J.kAT[hd, :, sl0:sl0 + ntile * 128]
                                    key = ("kAT", J.name)
                                elif kind == "qA":
                                    dst = J.qAT[hd, :, sl0 - OWN0:sl0 - OWN0 + ntile * 128]
                                    key = ("qAT", J.name)
                                elif kind == "kB":
                                    dst = J.kBT[hd, :, sl0:sl0 + ntile * 128]
                                    key = ("kBT", J.name)
                                else:
                                    dst = J.qBT[hd, :, sl0 - OWN0:sl0 - OWN0 + ntile * 128]
                                    key = ("qBT", J.name)
                                DMA("pool", dst, src, r=[("ostg", ob)], w=[key])
                        elif kind in ("vA", "vB"):
                            OSv = OS.rearrange("p (t n) -> p t n", t=16)
                            for t in tiles:
                                pbk = pcnt[0] % 2
                                pcnt[0] += 1
                                P = bank(2 + pbk)
                                for k in range(16):
                                    MM(P, hT[:, k, t * 128:(t + 1) * 128], W[:, k, :], start=(k == 0), stop=(k == 15),
                                       r=[("hT", t), ("wsl", wbuf)], w=[("psP", pbk)])
                                CP("act", OSv[:, t - t_lo, :], P, r=[("psP", pbk)], w=[("ostg", ob)])
                            sl0 = s0 + t_lo * 128
                            if kind == "vA":
                                dst = J.vA[sl0:sl0 + ntile * 128, gi * 512:(gi + 1) * 512]
                                key = ("vA", J.name)
                            else:
                                dst = J.vB[sl0:sl0 + ntile * 128, gi * 512:(gi + 1) * 512]
                                key = ("vB", J.name)
                            DMA("pool", dst.rearrange("(t p) n -> p t n", p=128), OSv[:, 0:ntile, :],
                                r=[("ostg", ob)], w=[key])
                        else:
                            OSv = OS.rearrange("p (c n) -> p c n", c=4)
                            gbase = (0 if kind == "gA" else 16) + gi * 4
                            for c in range(4):
                                for tg in range(t_lo // 4, t_hi // 4):
                                    pbk = pcnt[0] % 2
                                    pcnt[0] += 1
                                    P = bank(2 + pbk)
                                    for k in range(16):
                                        MM(P, W[:, k, c * 128:(c + 1) * 128], hT[:, k, tg * 512:(tg + 1) * 512],
                                           start=(k == 0), stop=(k == 15),
                                           r=hTr[tg * 4:tg * 4 + 4] + [("wsl", wbuf)], w=[("psP", pbk)])
                                    ACT(OSv[:, c, (tg * 4 - t_lo) * 128:(tg * 4 - t_lo) * 128 + 512], P, AF.Sigmoid,
                                        r=[("psP", pbk)], w=[("ostg", ob)])
                            sl0 = s0 + t_lo * 128 - OWN0
                            for c in range(4):
                                DMA("pool", J.gT[gbase + c, :, sl0:sl0 + ntile * 128], OSv[:, c, 0:ntile * 128],
                                    r=[("ostg", ob)], w=[("gT", J.name)])
            AR.pop()
            S.barrier()

        def phase2(J):
            AR.push()
            a0, a1 = J.akeys
            NK = a1 - a0
            nkt = NK // 128
            KT = [AR.alloc([NK], BF16) for _ in range(2)]
            VV = [AR.alloc([nkt, 128], BF16) for _ in range(2)]
            QT = [AR.alloc([J.T], BF16) for _ in range(2)]
            NPP = 6
            PP = [AR.alloc([1024], BF16) for _ in range(NPP)]
            tA = AR.alloc([1024], BF16)
            tB = AR.alloc([1024], BF16)
            acc = AR.alloc([1024], F32)
            accb = AR.alloc([1024], BF16)
            accl = AR.alloc([1024], BF16)
            rr = AR.alloc([1024], F32)
            ptail = [None]
            t0 = AR.alloc([512], F32)
            t1 = AR.alloc([512], F32)
            on = [AR.alloc([512], BF16) for _ in range(2)]
            SC = 0.125
            pc = [0]
            scnt = [0]
            for h in range(8):
                hb_ = h % 2
                DMA("sp", KT[hb_], J.kAT[h, :, a0:a1], r=[("kAT", J.name)], w=[("KT", hb_)])
                DMA("sp", VV[hb_], J.vA[a0:a1, h * 128:(h + 1) * 128].rearrange("(t p) e -> p t e", p=128),
                    r=[("vA", J.name)], w=[("VV", hb_)])
                DMA("sp", QT[hb_], J.qAT[h], r=[("qAT", J.name)], w=[("QT", hb_)])
                for qt in range(J.T // 512):
                    q0 = qt * 512
                    o0, o1 = bank(6), bank(7)

                    def score(kt):
                        sb = scnt[0] % 3
                        scnt[0] += 1
                        Sb = ps[:, sb * 1024:(sb + 1) * 1024]
                        MM(Sb[:, 0:512], KT[hb_][0:64, kt * 128:(kt + 1) * 128], QT[hb_][0:64, q0:q0 + 512],
                           r=[("KT", hb_), ("QT", hb_)], w=[("psS", sb)])
                        MM(Sb[:, 512:1024], KT[hb_][64:128, kt * 128:(kt + 1) * 128], QT[hb_][64:128, q0:q0 + 512],
                           r=[("KT", hb_), ("QT", hb_)], w=[("psS", sb)])
                        pb_ = pc[0] % NPP
                        pc[0] += 1
                        ACT(PP[pb_], Sb, AF.Exp, r=[("psS", sb)], w=[("PP", pb_)], scale=SC)
                        return pb_

                    def pv(kt, pb_):
                        st_, sp_ = (kt == 0), (kt == nkt - 1)
                        Pt = PP[pb_]
                        V = VV[hb_][:, kt, :]
                        MM(o0, V, Pt[:, 0:512], start=st_, stop=sp_, r=[("VV", hb_), ("PP", pb_)], w=["o0"])
                        MM(o1, V, Pt[:, 512:1024], start=st_, stop=sp_, r=[("VV", hb_), ("PP", pb_)], w=["o1"])

                    def dsum(kt, pl):
                        TT("dve", tA, PP[pl[0]], PP[pl[1]], ALU.add, r=[("PP", pl[0]), ("PP", pl[1])], w=["tA"])
                        TT("dve", tB, PP[pl[2]], PP[pl[3]], ALU.add, r=[("PP", pl[2]), ("PP", pl[3])], w=["tB"])
                        TT("dve", tA, tA, tB, ALU.add, r=["tA", "tB"], w=["tA"])
                        if kt == 3:
                            CP("dve", acc, tA, r=["tA"], w=["acc"])
                        else:
                            TT("dve", acc, acc, tA, ALU.add, r=["tA", "acc"], w=["acc"])

                    prev = None
                    pl = []
                    for kt in range(nkt):
                        if kt == 4 and ptail[0] is not None:
                            ptail[0]()
                            ptail[0] = None
                        cur = score(kt)
                        pl.append(cur)
                        if prev is not None:
                            pv(kt - 1, prev)
                        if kt % 4 == 3:
                            dsum(kt, pl)
                            pl = []
                        prev = cur
                    pv(nkt - 1, prev)
                    CP("dve", t0, o0, r=["o0"], w=["t0"])
                    CP("dve", t1, o1, r=["o1"], w=["t1"])
                    CP("dve", accb, acc, r=["acc"], w=["accb"])
                    TT("dve", acc, acc, accb, ALU.subtract, r=["acc", "accb"], w=["acc"])
                    CP("dve", accl, acc, r=["acc"], w=["accl"])

                    def _tail(h=h, q0=q0, ob_=(h * (J.T // 512) + qt) % 2):
                        sx = scnt[0] % 3
                        Sx = ps[:, sx * 1024:(sx + 1) * 1024]
                        kx = ("psS", sx)
                        MM(Sx[:, 0:512], ones, accb[:, 0:512], start=True, stop=False, r=["ones", "accb"], w=[kx])
                        MM(Sx[:, 0:512], ones, accl[:, 0:512], start=False, stop=True, r=["ones", "accl"], w=[kx])
                        MM(Sx[:, 512:1024], ones, accb[:, 512:1024], start=True, stop=False, r=["ones", "accb"], w=[kx])
                        MM(Sx[:, 512:1024], ones, accl[:, 512:1024], start=False, stop=True, r=["ones", "accl"], w=[kx])
                        ACT(rr, Sx, AF.Ln, r=[kx], w=["rr"])
                        ACT(rr, rr, AF.Exp, r=["rr"], w=["rr"], scale=-1.0)
                        TT("dve", t0, t0, rr[:, 0:512], ALU.mult, r=["t0", "rr"], w=["t0"])
                        TT("dve", t1, t1, rr[:, 512:1024], ALU.mult, r=["t1", "rr"], w=["t1"])
                        STT("dve", on[ob_], t1, neglam, t0, ALU.mult, ALU.add, r=["t0", "t1", "neglam"], w=[("on", ob_)])
                        DMA("act", J.oAT[h, :, q0:q0 + 512], on[ob_], r=[("on", ob_)], w=[("oAT", J.name)])
                    ptail[0] = _tail
            if ptail[0] is not None:
                ptail[0]()
                ptail[0] = None
            AR.pop()
            S.barrier()

        def phase3(J):
            AR.push()
            KB = AR.alloc([4, 4096], BF16)
            QB = AR.alloc([4, 2048], BF16)
            VB = AR.alloc([32, 512], BF16)
            vm = AR.alloc([J.nchunk * 3 * 32], F32)
            acc = AR.alloc([2, 4, 2048], F32)
            NB3 = 3
            Pb = [AR.alloc([256], BF16) for _ in range(NB3)]
            Pm = [AR.alloc([256], BF16) for _ in range(NB3)]
            obo = AR.alloc([4, 2048], BF16)
            mask2 = AR.alloc([256], BF16)
            CP("dve", mask2[:, 0:128], maskA, r=["maskA"], w=["mask2"])
            CP("dve", mask2[:, 128:256], maskB, r=["maskB"], w=["mask2"])
            m3 = mask2.rearrange("p (a c) -> p a c", a=2)
            DMA("sp", vm, J.vmask, w=["vm"])
            SCB = 128.0 ** -0.5
            cnt = [0]
            pendB = [None]
            for ci in range(J.nchunk):
                cb = ci * 2048
                for g, r_ in enumerate(B_GROUPS_R):
                    if pendB[0] is not None:
                        pendB[0]()
                        pendB[0] = None
                    for j in range(4):
                        hd = g * 4 + j
                        DMA("sp", KB[:, j, :], J.kBT[hd, :, cb:cb + 4096], r=[("kBT", J.name)], w=["KB"])
                        DMA("sp", QB[:, j, :], J.qBT[hd, :, cb:cb + 2048], r=[("qBT", J.name)], w=["QB"])
                    npr = 32 // r_
                    for rho in range(r_):
                        src = J.vB[cb:cb + 4096, g * 512:(g + 1) * 512].rearrange("(l r) n -> r l n", r=r_)[rho]
                        DMA("sp", VB[:, rho * npr:(rho + 1) * npr, :], src.rearrange("(t p) n -> p t n", p=128),
                            r=[("vB", J.name)], w=["VB"])
                    l0, l1 = 1024 // r_, 3072 // r_
                    for j in range(4):
                        for rho in range(r_):
                            for b in range(npr - 1):
                                lq0 = 64 + 128 * b
                                ja, jb = max(0, l0 - lq0), min(128, l1 - lq0)
                                if jb <= ja:
                                    continue
                                n = jb - ja
                                i_ = cnt[0] % NB3
                                cnt[0] += 1
                                psS = bank(i_, 256)
                                psND = bank(3 + i_, 256)
                                qs = rho + r_ * (lq0 + ja) - 1024
                                qap = QB[:, j, qs:qs + r_ * (n - 1) + 1:r_]
                                kA_ = rho + r_ * 128 * b
                                kB_ = rho + r_ * 128 * (b + 1)
                                kap = KB[:, j, kA_:kA_ + r_ * 127 + 1:r_]
                                kbp = KB[:, j, kB_:kB_ + r_ * 127 + 1:r_]
                                MM(psS[:, 0:n], kap, qap, r=["KB", "QB"], w=[("bS", i_)])
                                MM(psS[:, 128:128 + n], kbp, qap, r=["KB", "QB"], w=[("bS", i_)])
                                tA = rho * npr + b
                                tB = tA + 1
                                vbase = (ci * 3 + g) * 32
                                ACT(Pb[i_][:, 0:n], psS[:, 0:n], AF.Exp, r=[("bS", i_), "vm"], w=[("Pb", i_)],
                                    scale=SCB, bias=vm[:, vbase + tA:vbase + tA + 1])
                                ACT(Pb[i_][:, 128:128 + n], psS[:, 128:128 + n], AF.Exp, r=[("bS", i_), "vm"],
                                    w=[("Pb", i_)], scale=SCB, bias=vm[:, vbase + tB:vbase + tB + 1])
                                Pb3 = Pb[i_].rearrange("p (a c) -> p a c", a=2)[:, :, 0:n]
                                Pm3 = Pm[i_].rearrange("p (a c) -> p a c", a=2)[:, :, 0:n]
                                TT("dve", Pm3, Pb3, m3[:, :, ja:jb], ALU.mult, r=[("Pb", i_), "mask2"], w=[("Pm", i_)])

                                def _stB(n=n, i_=i_, psND=psND, tA=tA, tB=tB, j=j, qs=qs, r_=r_, g=g):
                                    MM(psND[:, 0:n], VB[:, tA, j * 128:(j + 1) * 128], Pm[i_][:, 0:n], start=True, stop=False,
                                       r=["VB", ("Pm", i_)], w=[("bN", i_)])
                                    MM(psND[:, 0:n], VB[:, tB, j * 128:(j + 1) * 128], Pm[i_][:, 128:128 + n], start=False,
                                       stop=True, r=["VB", ("Pm", i_)], w=[("bN", i_)])
                                    MM(psND[:, 128:128 + n], ones, Pm[i_][:, 0:n], start=True, stop=False,
                                       r=["ones", ("Pm", i_)], w=[("bN", i_)])
                                    MM(psND[:, 128:128 + n], ones, Pm[i_][:, 128:128 + n], start=False, stop=True,
                                       r=["ones", ("Pm", i_)], w=[("bN", i_)])
                                    aND = acc[:, :, j, qs:qs + r_ * (n - 1) + 1:r_]
                                    sND = psND.rearrange("p (a c) -> p a c", a=2)[:, :, 0:n]
                                    if g == 0:
                                        CP("dve", aND, sND, r=[("bN", i_)], w=["acc"])
                                    else:
                                        TT("dve", aND, aND, sND, ALU.add, r=[("bN", i_), "acc"], w=["acc"])
                                if pendB[0] is not None:
                                    pendB[0]()
                                pendB[0] = _stB
                if pendB[0] is not None:
                    pendB[0]()
                    pendB[0] = None
                accD = acc[:, 1].rearrange("p a n -> p (a n)")
                accN = acc[:, 0].rearrange("p a n -> p (a n)")
                ACT(accD, accD, AF.Ln, r=["acc"], w=["acc"])
                ACT(accD, accD, AF.Exp, r=["acc"], w=["acc"], scale=-1.0)
                TT("dve", obo.rearrange("p a n -> p (a n)"), accN, accD, ALU.mult, r=["acc"], w=["obo"])
                for j in range(4):
                    DMA("act", J.oBT[j, :, cb:cb + 2048], obo[:, j, :], r=["obo"], w=[("oBT", J.name)])
            AR.pop()
            S.barrier()

        def phase4(J):
            AR.push()
            gfc = AR.alloc([16], F32)
            gfin = AR.alloc([D], F32)
            xr = AR.alloc([4, D], F32)
            oa = AR.alloc([8, 512], BF16)
            obb = AR.alloc([4, 512], BF16)
            gsb = [AR.alloc([2, 512], BF16) for _ in range(2)]
            mhf = AR.alloc([8192], BF16)
            mh = mhf.rearrange("p (c n) -> p c n", c=16)
            yo = mhf.bitcast(F32).rearrange("p (a d) -> p a d", a=2)
            uT = AR.alloc([64, 512], BF16)
            wsl = [AR.alloc([16, 512], BF16) for _ in range(2)]
            tm1 = AR.alloc([512], F32)
            tm2 = AR.alloc([512], F32)
            hb2 = [AR.alloc([D], BF16) for _ in range(2)]
            ssum = AR.alloc([4], F32)
            DMA("sp", gfc, g_ffn, w=["gfc"])
            DMA("sp", gfin, g_fin.partition_broadcast(128)[:, 0, :], w=["gfin"])
            wcnt = [0]
            pcnt = [0]
            gcn = [0]
            BK = lambda i: ("bk", i)

            def slab(src, rkey, nk=16):
                wbuf = wcnt[0] % 2
                wcnt[0] += 1
                DMA("sp", wsl[wbuf][:, 0:nk, :], src, r=[rkey], w=[("wsl", wbuf)])
                return wsl[wbuf], ("wsl", wbuf)

            def load_pab(fg):
                wbuf = wcnt[0] % 2
                wcnt[0] += 1
                W = wsl[wbuf]
                wk = ("wsl", wbuf)
                DMA("sp", W[:, 0:8, :], wb_pa[:, :, fg * 512:(fg + 1) * 512], r=["wb_pa"], w=[wk])
                DMA("sp", W[:, 8:12, :], wb_pb[:, :, fg * 512:(fg + 1) * 512], r=["wb_pb"], w=[wk])
                return W, wk

            def rms4(dst_keys):
                for tt in range(4):
                    ACT(hb2[1], xr[:, tt, :], AF.Square, r=["xr"], w=[("hb2", 1), "ss"], accum=ssum[:, tt:tt + 1])
                ACT(ssum, ssum, AF.Ln, r=["ss"], w=["ss"], scale=1.0 / D, bias=1e-6)
                ACT(ssum, ssum, AF.Exp, r=["ss"], w=["ss"], scale=-0.5)

            for tg in range(J.T // 512):
                k0 = tg * 512
                DMA("sp", oa, J.oAT[:, :, k0:k0 + 512].rearrange("h e t -> e h t"), r=[("oAT", J.name)], w=["oa"])
                DMA("sp", obb, J.oBT[:, :, k0:k0 + 512].rearrange("h e t -> e h t"), r=[("oBT", J.name)], w=["obb"])
                nxt = load_pab(0)
                DMA("sp", xr, J.x[OWN0 + k0:OWN0 + k0 + 512, :].rearrange("(t p) d -> p t d", p=128), w=["xr"])
                for hb_ in range(2):
                    B0 = 4 if hb_ == 0 else 0
                    sq4 = hb2[0].rearrange("p (a n) -> p a n", a=4)
                    o4 = oa[:, hb_ * 4:(hb_ + 1) * 4, :]
                    o4f = o4.rearrange("p a n -> p (a n)")
                    bks = [BK(B0 + i) for i in range(4)]
                    TT("dve", sq4, o4, o4, ALU.mult, r=["oa"], w=[("hb2", 0)])
                    for i in range(4):
                        MM(bank(B0 + i), ones, sq4[:, i, :], r=["ones", ("hb2", 0)], w=[BK(B0 + i)])
                    pr = ps[:, B0 * 512:(B0 + 4) * 512]
                    ACT(pr, pr, AF.Ln, r=bks, w=bks, scale=1.0 / 128, bias=1e-5)
                    ACT(pr, pr, AF.Exp, r=bks, w=bks, scale=-0.5)
                    STT("dve", o4f, o4f, gcol, pr, ALU.mult, ALU.mult, r=["oa", "gcol"] + bks, w=["oa"])
                for fg in range(4):
                    W, wk = nxt
                    if fg + 1 < 4:
                        nxt = load_pab(fg + 1)
                    for fc in range(4):
                        f = fg * 4 + fc
                        gb_ = gcn[0] % 2
                        gcn[0] += 1
                        DMA("sp", gsb[gb_][:, 0, :], J.gT[f, :, k0:k0 + 512], r=[("gT", J.name)], w=[("gsb", gb_)])
                        DMA("sp", gsb[gb_][:, 1, :], J.gT[16 + f, :, k0:k0 + 512], r=[("gT", J.name)], w=[("gsb", gb_)])
                        b0_ = 2 * (f % 2)
                        pa, pbk = bank(b0_), bank(b0_ + 1)
                        for h in range(8):
                            MM(pa, W[:, h, fc * 128:(fc + 1) * 128], oa[:, h, :], start=(h == 0), stop=(h == 7),
                               r=[wk, "oa"], w=[BK(b0_)])
                        for j in range(4):
                            MM(pbk, W[:, 8 + j, fc * 128:(fc + 1) * 128], obb[:, j, :], start=(j == 0), stop=(j == 3),
                               r=[wk, "obb"], w=[BK(b0_ + 1)])
                        TT("dve", tm1, pa, gsb[gb_][:, 0, :], ALU.mult, r=[BK(b0_), ("gsb", gb_)], w=["tm1"])
                        TT("dve", tm2, pbk, gsb[gb_][:, 1, :], ALU.mult, r=[BK(b0_ + 1), ("gsb", gb_)], w=["tm2"])
                        TT("pool", mh[:, f, :], tm1, tm2, ALU.add, r=["tm1", "tm2"], w=["mh", ("yo", 0), ("yo", 1)])
                for og in range(4):
                    W, wk = slab(wb_out[og], ("wb_out", og))
                    for tt in range(4):
                        pbn = pcnt[0] % 2
                        pcnt[0] += 1
                        P = bank(2 + pbn)
                        for k in range(16):
                            MM(P, mh[:, k, tt * 128:(tt + 1) * 128], W[:, k, :], start=(k == 0), stop=(k == 15),
                               r=["mh", wk], w=[BK(2 + pbn)])
                        xs_ = xr[:, tt, og * 512:(og + 1) * 512]
                        TT("dve", xs_, xs_, P, ALU.add, r=[BK(2 + pbn), "xr"], w=["xr"])
                rms4(None)
                for tt in range(4):
                    hbb = hb2[tt % 2]
                    hk = ("hb2", tt % 2)
                    TSC("dve", hbb, xr[:, tt, :], ssum[:, tt:tt + 1], None, ALU.mult, r=["xr", "ss"], w=[hk])
                    for half in range(2):
                        pb = bank(half).bitcast(BF16)
                        for c in range(8):
                            cc = half * 8 + c
                            TR(pb[:, c * 128:(c + 1) * 128], hbb[:, cc * 128:(cc + 1) * 128],
                               r=[hk, "ident"], w=[BK(half)])
                        TT("dve", mh[:, half * 8:(half + 1) * 8, tt * 128:(tt + 1) * 128],
                           pb.rearrange("p (c n) -> p c n", c=8), bcast_inner(gfc[:, half * 8:(half + 1) * 8], 128),
                           ALU.mult, r=[BK(half), "gfc"], w=["mh"])
                for fb in range(16):
                    W, wk = slab(wb_1[fb], ("wb_1", fb))
                    for fc in range(4):
                        pbn = pcnt[0] % 2
                        pcnt[0] += 1
                        P = bank(2 + pbn)
                        for k in range(16):
                            MM(P, W[:, k, fc * 128:(fc + 1) * 128], mh[:, k, :], start=(k == 0), stop=(k == 15),
                               r=["mh", wk], w=[BK(2 + pbn)])
                        tq = tm1 if pbn == 0 else tm2
                        tk = "tm1" if pbn == 0 else "tm2"
                        ACT(tq, P, AF.Square, r=[BK(2 + pbn)], w=[tk])
                        STT("dve", uT[:, fb * 4 + fc, :], P, 0.0, tq, ALU.is_gt, ALU.mult,
                            r=[BK(2 + pbn), tk], w=["uT"])
                for og in range(4):
                    for qd in range(4):
                        W, wk = slab(wb_2[og, qd], ("wb_2", og, qd))
                        for tt in range(4):
                            P = bank(4 + tt)
                            for k in range(16):
                                kk = qd * 16 + k
                                MM(P, uT[:, kk, tt * 128:(tt + 1) * 128], W[:, k, :], start=(kk == 0), stop=(kk == 63),
                                   r=["uT", wk], w=[BK(4 + tt)])
                    for tt in range(4):
                        xs_ = xr[:, tt, og * 512:(og + 1) * 512]
                        TT("dve", xs_, xs_, bank(4 + tt), ALU.add, r=[BK(4 + tt), "xr"], w=["xr"])
                rms4(None)
                for tt in range(4):
                    a_ = tt % 2
                    STT("dve", yo[:, a_, :], xr[:, tt, :], ssum[:, tt:tt + 1], gfin, ALU.mult, ALU.mult,
                        r=["xr", "ss", "gfin"], w=["mh", ("yo", a_)])
                    DMA("act", J.y[k0 + tt * 128:k0 + (tt + 1) * 128, :], yo[:, a_, :], r=[("yo", a_)],
                        w=[("y", J.name)])
            AR.pop()
            S.barrier()

        for J in jobs:
            if 1 in phases:
                phase1(J)
        for J in jobs:
            if 2 in phases:
                phase2(J)
        for J in jobs:
            if 3 in phases:
                phase3(J)
        for J in jobs:
            if 4 in phases:
                phase4(J)
        stats = S.emit(nc, es)
    return nc, stats


def _rope_table(pos):
    pos = pos.astype(np.float32)
    out = np.zeros((pos.shape[0], 256), np.float32)
    for rot, off, nm in ((16, 0, 8), (32, 128, 4)):
        half = rot // 2
        inv = np.power(np.float32(500000.0), -np.arange(0, rot, 2, dtype=np.float32) / np.float32(rot)).astype(np.float32)
        ang = (pos[:, None] * inv[None, :]).astype(np.float32)
        out[:, off:off + 64] = np.tile(np.cos(ang), (1, nm))
        out[:, off + 64:off + 128] = np.tile(np.sin(ang), (1, nm))
    return out


def _vmask(valid_ext, nchunk):
    vm = np.zeros((128, nchunk, 3, 32), np.float32)
    i = np.arange(128)
    for ci in range(nchunk):
        cb = ci * 2048
        for g, r in enumerate(B_GROUPS_R):
            npr = 32 // r
            for rho in range(r):
                for jt in range(npr):
                    vm[:, ci, g, rho * npr + jt] = (valid_ext[cb + rho + r * (128 * jt + i)] - 1.0) * 30000.0
    return vm.reshape(128, nchunk * 3 * 32)


def _col_layout(g):
    return np.ascontiguousarray(np.asarray(g, np.float32).reshape(16, 128).T)


def make_in_maps(inputs, NSS, n_cores=8, jobs_enabled=("p", "s")):
    f32 = np.float32
    common = {
        "w_in": np.ascontiguousarray(inputs["w_in"][0], f32),
        "w_pa": np.ascontiguousarray(inputs["w_proj_a"][0], f32),
        "w_pb": np.ascontiguousarray(inputs["w_proj_b"][0], f32),
        "w_out": np.ascontiguousarray(inputs["w_out"][0], f32),
        "w_1": np.ascontiguousarray(inputs["w1"][0], f32),
        "w_2": np.ascontiguousarray(inputs["w2"][0], f32),
        "g_mix": _col_layout(inputs["norm_mix"][0]),
        "g_ffn": _col_layout(inputs["norm_ffn"][0]),
        "g_fin": np.ascontiguousarray(np.asarray(inputs["norm_final"], f32).reshape(1, D)),
        "lam_in": np.concatenate([np.asarray(inputs[k][0], f32) for k in
                                  ("lambda_q1", "lambda_k1", "lambda_q2", "lambda_k2")]).reshape(1, 256),
        "subln": np.ascontiguousarray(np.asarray(inputs["subln_g"][0], f32).reshape(128, 1)),
    }
    maps = []
    info = []
    per_seq = NSS // 4096
    for c in range(n_cores):
        m = dict(common)
        if "p" in jobs_enabled:
            xp = np.zeros((4096, D), f32)
            xp[1024:3072] = inputs["x_prompt"][c]
            pos = np.arange(4096) - 1024
            valid = ((pos >= 0) & (pos < 2048)).astype(f32)
            m["x_p"] = xp
            m["rope_p"] = _rope_table(np.clip(pos, 0, 2047))
            m["vmask_p"] = _vmask(valid, 1)
        if "s" in jobs_enabled:
            seq, c4 = c // per_seq, c % per_seq
            t0 = c4 * 4096
            pos_ext = t0 - 1024 + np.arange(6144)
            v_ext = (pos_ext >= 0) & (pos_ext < NSS)
            used = np.zeros(NSS, bool)
            used[pos_ext[v_ext]] = True
            far = np.nonzero(~used)[0]
            perm = np.empty(NSS, np.int64)
            ninv = int((~v_ext).sum())
            perm[:6144][v_ext] = pos_ext[v_ext]
            perm[:6144][~v_ext] = far[:ninv]
            perm[6144:] = far[ninv:]
            m["x_s"] = np.ascontiguousarray(np.asarray(inputs["x_sample"][seq], f32)[perm])
            m["rope_s"] = _rope_table(perm)
            m["vmask_s"] = _vmask(v_ext.astype(f32), 2)
            info.append((seq, t0))
        maps.append(m)
    return maps, info


_CACHE = {}


def kernel(**inputs):
    inputs = {k: np.asarray(v) for k, v in inputs.items()}
    NSS = inputs["x_sample"].shape[1]
    if NSS not in _CACHE:
        _CACHE[NSS] = build_program(NSS)[0]
    nc = _CACHE[NSS]
    maps, info = make_in_maps(inputs, NSS)
    res = run_bass_kernel_spmd(nc, maps, core_ids=list(range(8)))
    yp = np.stack([np.asarray(res.results[c]["y_p"], np.float32) for c in range(8)], axis=0)
    ys = np.zeros(inputs["x_sample"].shape, np.float32)
    for c, (seq, t0) in enumerate(info):
        ys[seq, t0:t0 + 4096] = np.asarray(res.results[c]["y_s"], np.float32)
    return (yp, ys)
```

```python
import numpy as np
import concourse.bass as bass
import concourse.mybir as mybir

F32 = mybir.dt.float32
BF16 = mybir.dt.bfloat16
AF = mybir.ActivationFunctionType
ALU = mybir.AluOpType
AX = mybir.AxisListType

ENGS = ("pe", "act", "dve", "pool", "sp")
DMA_RING = {"sp": 8, "pool": 6, "act": 4}


class Op:
    __slots__ = ("eng", "fn", "deps", "dma", "sig", "sigval", "dsem", "dval", "idx", "qidx", "cdeps")

    def __init__(self, eng, fn, dma):
        self.eng = eng
        self.fn = fn
        self.dma = dma
        self.deps = set()
        self.sig = False
        self.sigval = 0
        self.dsem = None
        self.dval = 0
        self.qidx = -1


class Sched:
    def __init__(self):
        self.ops = []
        self.last_w = {}
        self.readers = {}
        self.const = set()
        self.last_on = {e: None for e in ENGS}
        self.bar_deps = {e: set() for e in ENGS}
        self.dma_q = {q: [] for q in DMA_RING}

    def add(self, eng, fn, reads=(), writes=(), dma=False):
        op = Op(eng, fn, dma)
        op.idx = len(self.ops)
        deps = op.deps
        for b in reads:
            w = self.last_w.get(b)
            if w is not None:
                deps.add(w)
        for b in writes:
            w = self.last_w.get(b)
            if w is not None:
                deps.add(w)
            for r in self.readers.get(b, ()):
                deps.add(r)
        for b in reads:
            if b not in self.const:
                self.readers.setdefault(b, []).append(op.idx)
        for b in writes:
            self.last_w[b] = op.idx
            self.readers[b] = []
        if self.bar_deps[eng]:
            deps |= self.bar_deps[eng]
            self.bar_deps[eng] = set()
        deps.discard(op.idx)
        if dma:
            op.qidx = len(self.dma_q[eng])
            self.dma_q[eng].append(op.idx)
        self.ops.append(op)
        self.last_on[eng] = op.idx
        return op.idx

    def freeze(self, *keys):
        for k in keys:
            self.const.add(k)

    def barrier(self):
        allp = set()
        for e in ENGS:
            if self.last_on[e] is not None:
                allp.add(self.last_on[e])
        for q, lst in self.dma_q.items():
            for i in lst[-DMA_RING[q]:]:
                allp.add(i)
        for e in ENGS:
            self.bar_deps[e] |= allp
        self.last_w = {k: v for k, v in self.last_w.items() if k in self.const}
        self.readers = {}

    def emit(self, nc, es):
        ops = self.ops
        for op in ops:
            best = {}
            for d in op.deps:
                p = ops[d]
                if p.dma:
                    continue
                if p.eng == "pe" and op.eng == "pe" and not op.dma:
                    continue
                if p.eng not in best or best[p.eng] < d:
                    best[p.eng] = d
            op.cdeps = list(best.values())
            for d in op.cdeps:
                ops[d].sig = True
        cnt = {e: 0 for e in ENGS}
        for op in ops:
            if op.dma:
                continue
            if op.sig:
                cnt[op.eng] += 1
                op.sigval = cnt[op.eng]
        csem = {e: es.enter_context(nc.semaphore("c_" + e)) for e in ("pe", "act", "dve", "pool")}
        dsem = {q: [es.enter_context(nc.semaphore("d_%s%d" % (q, i))) for i in range(n)]
                for q, n in DMA_RING.items()}
        for q, lst in self.dma_q.items():
            n = DMA_RING[q]
            for j, i in enumerate(lst):
                ops[i].dsem = dsem[q][j % n]
                ops[i].dval = 16 * (j // n + 1)
        per_eng = {e: [op for op in ops if op.eng == e] for e in ENGS}
        stats = {e: [len(per_eng[e]), 0] for e in ENGS}

        def run(ename, eng):
            known = {}

            def wait(sem, val):
                k = id(sem)
                if known.get(k, 0) >= val:
                    return
                known[k] = val
                eng.wait_ge(sem, val)
                stats[ename][1] += 1

            last = None
            for op in per_eng[ename]:
                dm = {}
                for d in op.deps:
                    p = ops[d]
                    if p.dma:
                        k = id(p.dsem)
                        if k not in dm or dm[k][1] < p.dval:
                            dm[k] = (p.dsem, p.dval)
                for sem_, val_ in dm.values():
                    wait(sem_, val_)
                for d in op.cdeps:
                    p = ops[d]
                    wait(csem[p.eng], p.sigval)
                if op.dma:
                    if op.dval > 16:
                        wait(op.dsem, op.dval - 16)
                    ins = op.fn(eng)
                    ins.then_inc(op.dsem, 16)
                    last = op
                else:
                    ins = op.fn(eng)
                    if op.sig:
                        ins.then_inc(csem[ename], 1)
            if ename in self.dma_q:
                for i in self.dma_q[ename][-DMA_RING[ename]:]:
                    wait(ops[i].dsem, ops[i].dval)

        with nc.Block() as block:
            @block.tensor
            def _(e):
                run("pe", e)

            @block.scalar
            def _(e):
                run("act", e)

            @block.vector
            def _(e):
                run("dve", e)

            @block.gpsimd
            def _(e):
                run("pool", e)

            @block.sync
            def _(e):
                run("sp", e)
        return stats


class Arena:
    def __init__(self, raw, nbytes):
        self.raw = raw
        self.nbytes = nbytes
        self.off = 0
        self.marks = []

    def alloc(self, shape_free, dtype):
        esz = 2 if dtype == BF16 else 4
        n = int(np.prod(shape_free))
        nb = n * esz
        off = (self.off + 63) // 64 * 64
        assert off + nb <= self.nbytes, ("SBUF arena overflow", off, nb, self.nbytes)
        self.off = off + nb
        ap = self.raw[:, off:off + nb].bitcast(dtype)
        if len(shape_free) > 1:
            names = " ".join("d%d" % i for i in range(len(shape_free)))
            kw = {"d%d" % i: int(s) for i, s in enumerate(shape_free)}
            ap = ap.rearrange("p (%s) -> p %s" % (names, names), **kw)
        return ap

    def push(self):
        self.marks.append(self.off)

    def pop(self):
        self.off = self.marks.pop()

from contextlib import ExitStack
from concourse.bass_utils import run_bass_kernel_spmd

D = 2048
DFF = 8192
INW = 11776
U8 = mybir.dt.uint8
ARENA_BYTES = 186 * 1024
CG = {"qA": [0, 1], "kA": [2, 3], "vA": [4, 5], "qB": [6, 7, 8], "kB": [9, 10, 11],
      "vB": [12, 13, 14], "gA": [15, 16, 17, 18], "gB": [19, 20, 21, 22]}
NEED = {"own": {"qA", "kA", "vA", "qB", "kB", "vB", "gA", "gB"},
        "halo": {"kA", "vA", "kB", "vB"}, "pad": set(), "far": {"kA", "vA"}}
OWN0 = 1024
B_GROUPS_R = (1, 4, 16)


class Job:
    pass


def build_program(NSS, jobs_enabled=("p", "s"), phases=(1, 2, 3, 4), p0=True):
    nc = bass.Bass("TRN2", target_bir_lowering=False)

    def din(name, shape, dt=F32):
        return nc.dram_tensor(name, list(shape), dt, kind="ExternalInput").ap()

    def dout(name, shape, dt=F32):
        return nc.dram_tensor(name, list(shape), dt, kind="ExternalOutput").ap()

    def dscr(name, shape, dt=BF16):
        return nc.dram_tensor(name, list(shape), dt).ap()

    w_in = din("w_in", [D, INW])
    w_pa = din("w_pa", [1024, D])
    w_pb = din("w_pb", [512, D])
    w_out = din("w_out", [D, D])
    w_1 = din("w_1", [D, DFF])
    w_2 = din("w_2", [DFF, D])
    g_mix = din("g_mix", [128, 16])
    g_ffn = din("g_ffn", [128, 16])
    g_fin = din("g_fin", [1, D])
    lam_in = din("lam_in", [1, 256])
    subln = din("subln", [128, 1])

    jobs = []
    for nm in jobs_enabled:
        J = Job()
        J.name = nm
        if nm == "p":
            J.NS, J.T = 4096, 2048
            J.akeys = (1024, 3072)
            J.cls = ["pad"] * 8 + ["own"] * 16 + ["pad"] * 8
        else:
            J.NS, J.T = NSS, 4096
            J.akeys = (0, NSS)
            J.cls = ["halo"] * 8 + ["own"] * 32 + ["halo"] * 8 + ["far"] * ((NSS - 6144) // 128)
        J.EXT = J.T + 2048
        J.nchunk = J.T // 2048
        J.x = din("x_" + nm, [J.NS, D])
        J.rope = din("rope_" + nm, [J.NS, 256])
        J.vmask = din("vmask_" + nm, [128, J.nchunk * 3 * 32])
        J.y = dout("y_" + nm, [J.T, D])
        J.kAT = dscr("kAT_" + nm, [8, 128, J.NS])
        J.vA = dscr("vA_" + nm, [J.NS, 1024])
        J.qAT = dscr("qAT_" + nm, [8, 128, J.T])
        J.kBT = dscr("kBT_" + nm, [12, 128, J.EXT])
        J.qBT = dscr("qBT_" + nm, [12, 128, J.T])
        J.vB = dscr("vB_" + nm, [J.EXT, 1536])
        J.gT = dscr("gT_" + nm, [32, 128, J.T])
        J.oAT = dscr("oAT_" + nm, [8, 128, J.T])
        J.oBT = dscr("oBT_" + nm, [4, 128, J.T])
        jobs.append(J)

    wb_in = dscr("wb_in", [23, 128, 16, 512])
    wb_pa = dscr("wb_pa", [128, 8, D])
    wb_pb = dscr("wb_pb", [128, 4, D])
    wb_out = dscr("wb_out", [4, 128, 16, 512])
    wb_1 = dscr("wb_1", [16, 128, 16, 512])
    wb_2 = dscr("wb_2", [4, 4, 128, 16, 512])

    S = Sched()
    es = ExitStack()
    with es:
        raw = es.enter_context(nc.sbuf_tensor("arena", [128, ARENA_BYTES], U8))
        ps = es.enter_context(nc.psum_tensor("ps", [128, 4096], F32))
        AR = Arena(raw, ARENA_BYTES)

        def bank(i, n=512):
            return ps[:, i * 512:i * 512 + n]

        def MM(out, lhsT, rhs, start=True, stop=True, r=(), w=()):
            S.add("pe", lambda e: e.matmul(out, lhsT, rhs, start=start, stop=stop), r, w)

        def TR(out, in_, r=(), w=()):
            S.add("pe", lambda e: e.transpose(out, in_, ident), list(r), w)

        def ACT(out, in_, func, r=(), w=(), scale=1.0, bias=0.0, accum=None):
            if accum is None:
                S.add("act", lambda e: e.activation(out=out, in_=in_, func=func, bias=bias, scale=scale), r, w)
            else:
                S.add("act", lambda e: e.activation(out=out, in_=in_, func=func, bias=bias, scale=scale,
                                                    accum_out=accum), r, w)

        def DMA(q, out, in_, r=(), w=()):
            S.add(q, lambda e: e.dma_start(out=out, in_=in_), r, w, dma=True)

        def TT(eng, out, a, b, op, r=(), w=()):
            S.add(eng, lambda e: e.tensor_tensor(out, a, b, op), r, w)

        def TSC(eng, out, a, s1, s2, op0, op1=None, r=(), w=()):
            if op1 is None:
                S.add(eng, lambda e: e.tensor_scalar(out, a, s1, None, op0), r, w)
            else:
                S.add(eng, lambda e: e.tensor_scalar(out, a, s1, s2, op0, op1), r, w)

        def STT(eng, out, in0, scalar, in1, op0, op1, r=(), w=()):
            S.add(eng, lambda e: e.scalar_tensor_tensor(out, in0, scalar, in1, op0, op1), r, w)

        def CP(eng, out, in_, r=(), w=()):
            if eng == "act":
                S.add("act", lambda e: e.activation(out=out, in_=in_, func=AF.Copy), r, w)
            else:
                S.add(eng, lambda e: e.tensor_copy(out, in_), r, w)

        def bcast_mid(ap2d, reps):
            pat = [list(ap2d.ap[0])] + [[0, int(rp)] for rp in reps] + [list(ap2d.ap[-1])]
            return bass.AP(ap2d.tensor, ap2d.offset, pat)

        def bcast_inner(ap2d, n):
            pat = [list(ap2d.ap[0]), list(ap2d.ap[1]), [0, int(n)]]
            return bass.AP(ap2d.tensor, ap2d.offset, pat)

        ident = AR.alloc([128], BF16)
        ones = AR.alloc([128], BF16)
        maskA = AR.alloc([128], BF16)
        maskB = AR.alloc([128], BF16)
        iot = AR.alloc([128], F32)
        neglam = AR.alloc([1], F32)
        gcol = AR.alloc([1], F32)
        lamt = AR.alloc([256], F32)
        lamp = AR.alloc([128], F32)
        lams = AR.alloc([2], F32)
        S.add("pool", lambda e: e.iota(iot, [[1, 128]], channel_multiplier=-1, allow_small_or_imprecise_dtypes=True),
              (), ["iot"])
        TSC("dve", ident, iot, 0.0, None, ALU.is_equal, r=["iot"], w=["ident"])
        TSC("dve", maskA, iot, 0.0, None, ALU.is_le, r=["iot"], w=["maskA"])
        TSC("dve", maskB, iot, 0.0, None, ALU.is_ge, r=["iot"], w=["maskB"])
        S.add("pool", lambda e: e.memset(ones, 1.0), (), ["ones"])
        DMA("sp", lamt, lam_in.partition_broadcast(128)[:, 0, :], w=["lamt"])
        DMA("sp", gcol, subln, w=["gcol0"])
        TT("dve", lamp[:, 0:64], lamt[:, 0:64], lamt[:, 64:128], ALU.mult, r=["lamt"], w=["lamp"])
        TT("dve", lamp[:, 64:128], lamt[:, 128:192], lamt[:, 192:256], ALU.mult, r=["lamt"], w=["lamp"])
        S.add("dve", lambda e: e.reduce_sum(lams, lamp.rearrange("p (a b) -> p a b", a=2), AX.X), ["lamp"], ["lams"])
        ACT(lams, lams, AF.Exp, r=["lams"], w=["lams"])
        TT("dve", neglam, lams[:, 1:2], lams[:, 0:1], ALU.subtract, r=["lams"], w=["neglam"])
        TSC("dve", neglam, neglam, -0.2, None, ALU.add, r=["neglam"], w=["neglam"])
        TSC("dve", gcol, gcol, 0.8, None, ALU.mult, r=["gcol0"], w=["gcol"])
        S.freeze("ident", "ones", "maskA", "maskB", "neglam", "gcol")

        def phase0():
            AR.push()
            stf = [AR.alloc([4, 512], F32) for _ in range(4)]
            stb = [AR.alloc([4, 512], BF16) for _ in range(4)]
            cnt = [0]

            def cast(dst, src, key, nk=4):
                i = cnt[0] % 4
                ceng = "dve" if cnt[0] % 2 == 0 else "act"
                cnt[0] += 1
                DMA("sp", stf[i][:, 0:nk, :], src, w=[("stf", i)])
                CP(ceng, stb[i][:, 0:nk, :], stf[i][:, 0:nk, :], r=[("stf", i)], w=[("stb", i)])
                DMA("pool", dst, stb[i][:, 0:nk, :], r=[("stb", i)], w=[key])

            def rows(wap, r0, c0):
                return wap[r0:r0 + 512, c0:c0 + 512].rearrange("(k p) n -> p k n", p=128)

            for cg in ([2, 3, 4, 5, 9, 10, 11, 12, 13, 14, 0, 1, 6, 7, 8] + list(range(15, 23))):
                for kq in range(4):
                    cast(wb_in[cg][:, kq * 4:(kq + 1) * 4, :], rows(w_in, kq * 512, cg * 512), ("wb_in", cg))
            for fg in range(4):
                for hq in range(2):
                    cast(wb_pa[:, hq * 4:(hq + 1) * 4, fg * 512:(fg + 1) * 512],
                         w_pa[hq * 512:(hq + 1) * 512, fg * 512:(fg + 1) * 512].rearrange("(h e) f -> e h f", e=128), "wb_pa")
                cast(wb_pb[:, :, fg * 512:(fg + 1) * 512],
                     w_pb[:, fg * 512:(fg + 1) * 512].rearrange("(h e) f -> e h f", e=128), "wb_pb")
            for og in range(4):
                for kq in range(4):
                    cast(wb_out[og][:, kq * 4:(kq + 1) * 4, :], rows(w_out, kq * 512, og * 512), ("wb_out", og))
            for fb in range(16):
                for kq in range(4):
                    cast(wb_1[fb][:, kq * 4:(kq + 1) * 4, :], rows(w_1, kq * 512, fb * 512), ("wb_1", fb))
            for og in range(4):
                for qd in range(4):
                    for kq in range(4):
                        cast(wb_2[og, qd][:, kq * 4:(kq + 1) * 4, :], rows(w_2, qd * 2048 + kq * 512, og * 512),
                             ("wb_2", og, qd))
            AR.pop()
            S.barrier()

        if p0:
            phase0()

        def phase1(J):
            AR.push()
            gmc = AR.alloc([16], F32)
            hT = AR.alloc([16, 2048], BF16)
            xt = [AR.alloc([D], F32) for _ in range(2)]
            hb = [AR.alloc([D], BF16) for _ in range(2)]
            sqj = AR.alloc([D], BF16)
            ssum = [AR.alloc([1], F32) for _ in range(2)]
            rp = AR.alloc([16, 256], F32)
            tmx = AR.alloc([128], F32)
            wsl = [AR.alloc([16, 512], BF16) for _ in range(2)]
            zb = [AR.alloc([512], BF16) for _ in range(2)]
            tmc = AR.alloc([128], F32)
            tms = AR.alloc([128], F32)
            ostg = [AR.alloc([8192], BF16) for _ in range(2)]
            DMA("sp", gmc, g_mix, w=["gmc"])
            if "pad" in J.cls:
                S.add("pool", lambda e: e.memset(ostg[0], 0.0), (), [("ostg", 0)])
                zt = ostg[0]
                for s0_, s1_ in ((0, 1024), (J.NS - 1024, J.NS)):
                    for hd in range(12):
                        DMA("sp", J.kBT[hd, :, s0_:s1_], zt[:, 0:1024], r=[("ostg", 0)], w=[("kBT", J.name)])
                    for t4 in range(2):
                        DMA("sp", J.vB[s0_ + t4 * 512:s0_ + (t4 + 1) * 512, :].rearrange("(t p) n -> p t n", p=128),
                            zt[:, 0:6144].rearrange("p (t n) -> p t n", t=4), r=[("ostg", 0)], w=[("vB", J.name)])
            nst = J.NS // 2048
            wcnt = [0]
            ocnt = [0]
            pcnt = [0]
            pendP1 = [None]

            def flushP1():
                if pendP1[0] is not None:
                    f_ = pendP1[0]
                    pendP1[0] = None
                    f_()

            for st in range(nst):
                s0 = st * 2048
                cls = J.cls[st * 16:(st + 1) * 16]
                DMA("sp", rp, J.rope[s0:s0 + 2048, :].rearrange("(t p) c -> p t c", p=128), w=["rp"])
                flushP1()
                for t in range(16):
                    if cls[t] == "pad":
                        continue
                    b = t % 2
                    DMA("sp", xt[b], J.x[s0 + t * 128:s0 + (t + 1) * 128, :], w=[("xt", b)])
                    ACT(sqj, xt[b], AF.Square, r=[("xt", b)], w=["sqj", ("ss", b)], accum=ssum[b])
                    ACT(ssum[b], ssum[b], AF.Ln, r=[("ss", b)], w=[("ss", b)], scale=1.0 / D, bias=1e-6)
                    ACT(ssum[b], ssum[b], AF.Exp, r=[("ss", b)], w=[("ss", b)], scale=-0.5)
                    TSC("dve", hb[b], xt[b], ssum[b], None, ALU.mult, r=[("xt", b), ("ss", b)], w=[("hb", b)])
                    for half in range(2):
                        pb = bank(half).bitcast(BF16)
                        for c in range(8):
                            cc = half * 8 + c
                            TR(pb[:, c * 128:(c + 1) * 128], hb[b][:, cc * 128:(cc + 1) * 128],
                               r=[("hb", b), "ident"], w=[("psT", half)])
                        TT("dve", hT[:, half * 8:(half + 1) * 8, t * 128:(t + 1) * 128],
                           pb.rearrange("p (c n) -> p c n", c=8), bcast_inner(gmc[:, half * 8:(half + 1) * 8], 128),
                           ALU.mult, r=[("psT", half), "gmc"], w=[("hT", t)])
                hTr = [("hT", t) for t in range(16)]
                import os as _os
                _kinds = _os.environ.get("DBG_KINDS", "kA,vA,kB,vB,qA,qB,gA,gB").split(",")
                for kind in ("kA", "vA", "kB", "vB", "qA", "qB", "gA", "gB"):
                    if kind not in _kinds:
                        continue
                    tiles = [t for t in range(16) if kind in NEED[cls[t]]]
                    if not tiles:
                        continue
                    t_lo, t_hi = tiles[0], tiles[-1] + 1
                    assert tiles == list(range(t_lo, t_hi))
                    ntile = t_hi - t_lo
                    for gi, cg in enumerate(CG[kind]):
                        wbuf = wcnt[0] % 2
                        wcnt[0] += 1
                        W = wsl[wbuf]
                        DMA("sp", W, wb_in[cg], r=[("wb_in", cg)], w=[("wsl", wbuf)])
                        ob = ocnt[0] % 2
                        ocnt[0] += 1
                        OS = ostg[ob]
                        if kind in ("qA", "kA", "qB", "kB"):
                            isA = kind in ("qA", "kA")
                            nm_, hf = (8, 8) if isA else (4, 16)
                            co = 0 if isA else 128
                            OSv = OS.rearrange("p (c n) -> p c n", c=4)
                            pend = [None]
                            for t in tiles:
                                pbk = pcnt[0] % 2
                                pcnt[0] += 1
                                P = bank(2 + pbk)
                                for k in range(16):
                                    MM(P, hT[:, k, t * 128:(t + 1) * 128], W[:, k, :], start=(k == 0), stop=(k == 15),
                                       r=[("hT", t), ("wsl", wbuf)], w=[("psP", pbk)])
                                flushP1()
                                Z = zb[pbk]
                                CP("act", Z, P, r=[("psP", pbk)], w=[("zb", pbk)])
                                Pm_ = P.rearrange("p (m d) -> p m d", m=nm_)
                                nh = nm_ * hf
                                tx3 = tmx.rearrange("p (m d) -> p m d", m=nm_)
                                CP("act", tx3, Pm_[:, :, 0:2 * hf], r=[("psP", pbk)], w=["tmx"])
                                x1 = tx3[:, :, 0:hf]
                                x2 = tx3[:, :, hf:2 * hf]
                                cos3 = rp[:, t, co:co + nh].rearrange("p (m d) -> p m d", m=nm_)
                                sin3 = rp[:, t, co + nh:co + 2 * nh].rearrange("p (m d) -> p m d", m=nm_)
                                t1 = tmc[:, 0:nh].rearrange("p (m d) -> p m d", m=nm_)
                                t3 = tmc[:, nh:2 * nh].rearrange("p (m d) -> p m d", m=nm_)
                                t2 = tms[:, 0:nh].rearrange("p (m d) -> p m d", m=nm_)
                                t4 = tms[:, nh:2 * nh].rearrange("p (m d) -> p m d", m=nm_)
                                TT("dve", t1, x1, cos3, ALU.mult, r=["tmx", "rp"], w=["tmc"])
                                TT("dve", t3, x2, cos3, ALU.mult, r=["tmx", "rp"], w=["tmc"])
                                TT("dve", t2, x2, sin3, ALU.mult, r=["tmx", "rp"], w=["tms"])
                                TT("dve", t4, x1, sin3, ALU.mult, r=["tmx", "rp"], w=["tms"])
                                Zv = Z.rearrange("p (m d) -> p m d", m=nm_)
                                TT("dve", Zv[:, :, 0:hf], t1, t2, ALU.subtract, r=["tmc", "tms"], w=[("zb", pbk)])
                                TT("dve", Zv[:, :, hf:2 * hf], t3, t4, ALU.add, r=["tmc", "tms"], w=[("zb", pbk)])
                                def _tr(t=t, pbk=pbk, Z=Z, OSv=OSv, ob=ob, t_lo=t_lo):
                                    ptb = bank(4 + pbk).bitcast(BF16)[:, 0:512]
                                    for c in range(4):
                                        TR(ptb[:, c * 128:(c + 1) * 128], Z[:, c * 128:(c + 1) * 128],
                                           r=[("zb", pbk), "ident"], w=[("psZ", pbk)])
                                    CP("dve", OSv[:, :, (t - t_lo) * 128:(t - t_lo + 1) * 128],
                                       ptb.rearrange("p (c n) -> p c n", c=4), r=[("psZ", pbk)], w=[("ostg", ob)])
                                if pend[0] is not None:
                                    pend[0]()
                                pend[0] = _tr
                            def _fin(last=pend[0], gi=gi, OSv=OSv, ob=ob, kind=kind, ntile=ntile, t_lo=t_lo, s0=s0):
                                if last is not None:
                                    last()
                                for c in range(4):
                                    hd = gi * 4 + c
                                    src = OSv[:, c, 0:ntile * 128]
                                    sl0 = s0 + t_lo * 128
                                    if kind == "kA":
                                        dst = J.kAT[hd, :, sl0:sl0 + ntile * 128]
                                        key = ("kAT", J.name)
                                    elif kind == "qA":
                                        dst = J.qAT[hd, :, sl0 - OWN0:sl0 - OWN0 + ntile * 128]
                                        key = ("qAT", J.name)
                                    elif kind == "kB":
                                        dst = J.kBT[hd, :, sl0:sl0 + ntile * 128]
                                        key = ("kBT", J.name)
                                    else:
                                        dst = J.qBT[hd, :, sl0 - OWN0:sl0 - OWN0 + ntile * 128]
                                        key = ("qBT", J.name)
                                    DMA("pool", dst, src, r=[("ostg", ob)], w=[key])
                            pend[0] = None
                            flushP1()
                            pendP1[0] = _fin
                        elif kind in ("vA", "vB"):
                            OSv = OS.rearrange("p (t n) -> p t n", t=16)
                            for t in tiles:
                                pbk = pcnt[0] % 2
                                pcnt[0] += 1
                                P = bank(2 + pbk)
                                for k in range(16):
                                    MM(P, hT[:, k, t * 128:(t + 1) * 128], W[:, k, :], start=(k == 0), stop=(k == 15),
                                       r=[("hT", t), ("wsl", wbuf)], w=[("psP", pbk)])
                                flushP1()
                                CP("act", OSv[:, t - t_lo, :], P, r=[("psP", pbk)], w=[("ostg", ob)])
                            sl0 = s0 + t_lo * 128
                            if kind == "vA":
                                dst = J.vA[sl0:sl0 + ntile * 128, gi * 512:(gi + 1) * 512]
                                key = ("vA", J.name)
                            else:
                                dst = J.vB[sl0:sl0 + ntile * 128, gi * 512:(gi + 1) * 512]
                                key = ("vB", J.name)
                            DMA("pool", dst.rearrange("(t p) n -> p t n", p=128), OSv[:, 0:ntile, :],
                                r=[("ostg", ob)], w=[key])
                        else:
                            OSv = OS.rearrange("p (c n) -> p c n", c=4)
                            gbase = (0 if kind == "gA" else 16) + gi * 4
                            for c in range(4):
                                for tg in range(t_lo // 4, t_hi // 4):
                                    pbk = pcnt[0] % 2
                                    pcnt[0] += 1
                                    P = bank(2 + pbk)
                                    for k in range(16):
                                        MM(P, W[:, k, c * 128:(c + 1) * 128], hT[:, k, tg * 512:(tg + 1) * 512],
                                           start=(k == 0), stop=(k == 15),
                                           r=hTr[tg * 4:tg * 4 + 4] + [("wsl", wbuf)], w=[("psP", pbk)])
                                    flushP1()
                                    ACT(OSv[:, c, (tg * 4 - t_lo) * 128:(tg * 4 - t_lo) * 128 + 512], P, AF.Sigmoid,
                                        r=[("psP", pbk)], w=[("ostg", ob)])
                            sl0 = s0 + t_lo * 128 - OWN0
                            for c in range(4):
                                DMA("pool", J.gT[gbase + c, :, sl0:sl0 + ntile * 128], OSv[:, c, 0:ntile * 128],
                                    r=[("ostg", ob)], w=[("gT", J.name)])
            flushP1()
            AR.pop()
            S.barrier()

        def phase2(J):
            AR.push()
            a0, a1 = J.akeys
            NK = a1 - a0
            nkt = NK // 128
            KT = [AR.alloc([NK], BF16) for _ in range(2)]
            VV = [AR.alloc([nkt, 128], BF16) for _ in range(2)]
            QT = [AR.alloc([J.T], BF16) for _ in range(2)]
            NPP = 6
            PP = [AR.alloc([1024], BF16) for _ in range(NPP)]
            tA = AR.alloc([1024], BF16)
            tB = AR.alloc([1024], BF16)
            acc = AR.alloc([1024], F32)
            accb = AR.alloc([1024], BF16)
            accl = AR.alloc([1024], BF16)
            rr = AR.alloc([1024], F32)
            ptail = [None]
            t0 = AR.alloc([512], F32)
            t1 = AR.alloc([512], F32)
            on = [AR.alloc([512], BF16) for _ in range(2)]
            SC = 0.125
            pc = [0]
            scnt = [0]
            for h in range(8):
                hb_ = h % 2
                DMA("sp", KT[hb_], J.kAT[h, :, a0:a1], r=[("kAT", J.name)], w=[("KT", hb_)])
                DMA("sp", VV[hb_], J.vA[a0:a1, h * 128:(h + 1) * 128].rearrange("(t p) e -> p t e", p=128),
                    r=[("vA", J.name)], w=[("VV", hb_)])
                DMA("sp", QT[hb_], J.qAT[h], r=[("qAT", J.name)], w=[("QT", hb_)])
                for qt in range(J.T // 512):
                    q0 = qt * 512
                    o0, o1 = bank(6), bank(7)

                    def score(kt):
                        sb = scnt[0] % 3
                        scnt[0] += 1
                        Sb = ps[:, sb * 1024:(sb + 1) * 1024]
                        MM(Sb[:, 0:512], KT[hb_][0:64, kt * 128:(kt + 1) * 128], QT[hb_][0:64, q0:q0 + 512],
                           r=[("KT", hb_), ("QT", hb_)], w=[("psS", sb)])
                        MM(Sb[:, 512:1024], KT[hb_][64:128, kt * 128:(kt + 1) * 128], QT[hb_][64:128, q0:q0 + 512],
                           r=[("KT", hb_), ("QT", hb_)], w=[("psS", sb)])
                        pb_ = pc[0] % NPP
                        pc[0] += 1
                        ACT(PP[pb_], Sb, AF.Exp, r=[("psS", sb)], w=[("PP", pb_)], scale=SC)
                        return pb_

                    def pv(kt, pb_):
                        st_, sp_ = (kt == 0), (kt == nkt - 1)
                        Pt = PP[pb_]
                        V = VV[hb_][:, kt, :]
                        MM(o0, V, Pt[:, 0:512], start=st_, stop=sp_, r=[("VV", hb_), ("PP", pb_)], w=["o0"])
                        MM(o1, V, Pt[:, 512:1024], start=st_, stop=sp_, r=[("VV", hb_), ("PP", pb_)], w=["o1"])

                    def dsum(kt, pl):
                        TT("dve", tA, PP[pl[0]], PP[pl[1]], ALU.add, r=[("PP", pl[0]), ("PP", pl[1])], w=["tA"])
                        TT("dve", tB, PP[pl[2]], PP[pl[3]], ALU.add, r=[("PP", pl[2]), ("PP", pl[3])], w=["tB"])
                        TT("dve", tA, tA, tB, ALU.add, r=["tA", "tB"], w=["tA"])
                        if kt == 3:
                            CP("dve", acc, tA, r=["tA"], w=["acc"])
                        else:
                            TT("dve", acc, acc, tA, ALU.add, r=["tA", "acc"], w=["acc"])

                    prev = None
                    pl = []
                    for kt in range(nkt):
                        if kt == 4 and ptail[0] is not None:
                            ptail[0]()
                            ptail[0] = None
                        cur = score(kt)
                        pl.append(cur)
                        if prev is not None:
                            pv(kt - 1, prev)
                        if kt % 4 == 3:
                            dsum(kt, pl)
                            pl = []
                        prev = cur
                    pv(nkt - 1, prev)
                    CP("dve", t0, o0, r=["o0"], w=["t0"])
                    CP("dve", t1, o1, r=["o1"], w=["t1"])
                    CP("dve", accb, acc, r=["acc"], w=["accb"])
                    TT("dve", acc, acc, accb, ALU.subtract, r=["acc", "accb"], w=["acc"])
                    CP("dve", accl, acc, r=["acc"], w=["accl"])

                    def _tail(h=h, q0=q0, ob_=(h * (J.T // 512) + qt) % 2):
                        sx = scnt[0] % 3
                        Sx = ps[:, sx * 1024:(sx + 1) * 1024]
                        kx = ("psS", sx)
                        MM(Sx[:, 0:512], ones, accb[:, 0:512], start=True, stop=False, r=["ones", "accb"], w=[kx])
                        MM(Sx[:, 0:512], ones, accl[:, 0:512], start=False, stop=True, r=["ones", "accl"], w=[kx])
                        MM(Sx[:, 512:1024], ones, accb[:, 512:1024], start=True, stop=False, r=["ones", "accb"], w=[kx])
                        MM(Sx[:, 512:1024], ones, accl[:, 512:1024], start=False, stop=True, r=["ones", "accl"], w=[kx])
                        ACT(rr, Sx, AF.Ln, r=[kx], w=["rr"])
                        ACT(rr, rr, AF.Exp, r=["rr"], w=["rr"], scale=-1.0)
                        TT("dve", t0, t0, rr[:, 0:512], ALU.mult, r=["t0", "rr"], w=["t0"])
                        TT("dve", t1, t1, rr[:, 512:1024], ALU.mult, r=["t1", "rr"], w=["t1"])
                        STT("dve", on[ob_], t1, neglam, t0, ALU.mult, ALU.add, r=["t0", "t1", "neglam"], w=[("on", ob_)])
                        DMA("act", J.oAT[h, :, q0:q0 + 512], on[ob_], r=[("on", ob_)], w=[("oAT", J.name)])
                    ptail[0] = _tail
            if ptail[0] is not None:
                ptail[0]()
                ptail[0] = None
            AR.pop()
            S.barrier()

        def phase3(J):
            AR.push()
            KB = AR.alloc([4, 4096], BF16)
            QB = AR.alloc([4, 2048], BF16)
            VB = AR.alloc([32, 512], BF16)
            vm = AR.alloc([J.nchunk * 3 * 32], F32)
            acc = AR.alloc([2, 4, 2048], F32)
            NB3 = 3
            Pb = [AR.alloc([256], BF16) for _ in range(NB3)]
            Pm = [AR.alloc([256], BF16) for _ in range(NB3)]
            obo = AR.alloc([4, 2048], BF16)
            mask2 = AR.alloc([256], BF16)
            CP("dve", mask2[:, 0:128], maskA, r=["maskA"], w=["mask2"])
            CP("dve", mask2[:, 128:256], maskB, r=["maskB"], w=["mask2"])
            m3 = mask2.rearrange("p (a c) -> p a c", a=2)
            DMA("sp", vm, J.vmask, w=["vm"])
            SCB = 128.0 ** -0.5
            cnt = [0]
            pendB = [None]
            for ci in range(J.nchunk):
                cb = ci * 2048
                for g, r_ in enumerate(B_GROUPS_R):
                    if pendB[0] is not None:
                        pendB[0]()
                        pendB[0] = None
                    for j in range(4):
                        hd = g * 4 + j
                        DMA("sp", KB[:, j, :], J.kBT[hd, :, cb:cb + 4096], r=[("kBT", J.name)], w=["KB"])
                        DMA("sp", QB[:, j, :], J.qBT[hd, :, cb:cb + 2048], r=[("qBT", J.name)], w=["QB"])
                    npr = 32 // r_
                    for rho in range(r_):
                        src = J.vB[cb:cb + 4096, g * 512:(g + 1) * 512].rearrange("(l r) n -> r l n", r=r_)[rho]
                        DMA("sp", VB[:, rho * npr:(rho + 1) * npr, :], src.rearrange("(t p) n -> p t n", p=128),
                            r=[("vB", J.name)], w=["VB"])
                    l0, l1 = 1024 // r_, 3072 // r_
                    for j in range(4):
                        for rho in range(r_):
                            for b in range(npr - 1):
                                lq0 = 64 + 128 * b
                                ja, jb = max(0, l0 - lq0), min(128, l1 - lq0)
                                if jb <= ja:
                                    continue
                                n = jb - ja
                                i_ = cnt[0] % NB3
                                cnt[0] += 1
                                psS = bank(i_, 256)
                                psND = bank(3 + i_, 256)
                                qs = rho + r_ * (lq0 + ja) - 1024
                                qap = QB[:, j, qs:qs + r_ * (n - 1) + 1:r_]
                                kA_ = rho + r_ * 128 * b
                                kB_ = rho + r_ * 128 * (b + 1)
                                kap = KB[:, j, kA_:kA_ + r_ * 127 + 1:r_]
                                kbp = KB[:, j, kB_:kB_ + r_ * 127 + 1:r_]
                                MM(psS[:, 0:n], kap, qap, r=["KB", "QB"], w=[("bS", i_)])
                                MM(psS[:, 128:128 + n], kbp, qap, r=["KB", "QB"], w=[("bS", i_)])
                                tA = rho * npr + b
                                tB = tA + 1
                                vbase = (ci * 3 + g) * 32
                                ACT(Pb[i_][:, 0:n], psS[:, 0:n], AF.Exp, r=[("bS", i_), "vm"], w=[("Pb", i_)],
                                    scale=SCB, bias=vm[:, vbase + tA:vbase + tA + 1])
                                ACT(Pb[i_][:, 128:128 + n], psS[:, 128:128 + n], AF.Exp, r=[("bS", i_), "vm"],
                                    w=[("Pb", i_)], scale=SCB, bias=vm[:, vbase + tB:vbase + tB + 1])
                                Pb3 = Pb[i_].rearrange("p (a c) -> p a c", a=2)[:, :, 0:n]
                                Pm3 = Pm[i_].rearrange("p (a c) -> p a c", a=2)[:, :, 0:n]
                                TT("dve", Pm3, Pb3, m3[:, :, ja:jb], ALU.mult, r=[("Pb", i_), "mask2"], w=[("Pm", i_)])

                                def _stB(n=n, i_=i_, psND=psND, tA=tA, tB=tB, j=j, qs=qs, r_=r_, g=g):
                                    MM(psND[:, 0:n], VB[:, tA, j * 128:(j + 1) * 128], Pm[i_][:, 0:n], start=True, stop=False,
                                       r=["VB", ("Pm", i_)], w=[("bN", i_)])
                                    MM(psND[:, 0:n], VB[:, tB, j * 128:(j + 1) * 128], Pm[i_][:, 128:128 + n], start=False,
                                       stop=True, r=["VB", ("Pm", i_)], w=[("bN", i_)])
                                    MM(psND[:, 128:128 + n], ones, Pm[i_][:, 0:n], start=True, stop=False,
                                       r=["ones", ("Pm", i_)], w=[("bN", i_)])
                                    MM(psND[:, 128:128 + n], ones, Pm[i_][:, 128:128 + n], start=False, stop=True,
                                       r=["ones", ("Pm", i_)], w=[("bN", i_)])
                                    aND = acc[:, :, j, qs:qs + r_ * (n - 1) + 1:r_]
                                    sND = psND.rearrange("p (a c) -> p a c", a=2)[:, :, 0:n]
                                    if g == 0:
                                        CP("dve", aND, sND, r=[("bN", i_)], w=["acc"])
                                    else:
                                        TT("dve", aND, aND, sND, ALU.add, r=[("bN", i_), "acc"], w=["acc"])
                                if pendB[0] is not None:
                                    pendB[0]()
                                pendB[0] = _stB
                if pendB[0] is not None:
                    pendB[0]()
                    pendB[0] = None
                accD = acc[:, 1].rearrange("p a n -> p (a n)")
                accN = acc[:, 0].rearrange("p a n -> p (a n)")
                ACT(accD, accD, AF.Ln, r=["acc"], w=["acc"])
                ACT(accD, accD, AF.Exp, r=["acc"], w=["acc"], scale=-1.0)
                TT("dve", obo.rearrange("p a n -> p (a n)"), accN, accD, ALU.mult, r=["acc"], w=["obo"])
                for j in range(4):
                    DMA("act", J.oBT[j, :, cb:cb + 2048], obo[:, j, :], r=["obo"], w=[("oBT", J.name)])
            AR.pop()
            S.barrier()

        def phase4(J):
            AR.push()
            gfc = AR.alloc([16], F32)
            gfin = AR.alloc([D], F32)
            xr = AR.alloc([4, D], F32)
            oa = AR.alloc([8, 512], BF16)
            obb = AR.alloc([4, 512], BF16)
            gsb = [AR.alloc([2, 512], BF16) for _ in range(2)]
            mhf = AR.alloc([8192], BF16)
            mh = mhf.rearrange("p (c n) -> p c n", c=16)
            yo = mhf.bitcast(F32).rearrange("p (a d) -> p a d", a=2)
            uT = AR.alloc([64, 512], BF16)
            wsl = [AR.alloc([16, 512], BF16) for _ in range(2)]
            tm1 = AR.alloc([512], F32)
            tm2 = AR.alloc([512], F32)
            hb2 = [AR.alloc([D], BF16) for _ in range(2)]
            ssum = AR.alloc([4], F32)
            DMA("sp", gfc, g_ffn, w=["gfc"])
            DMA("sp", gfin, g_fin.partition_broadcast(128)[:, 0, :], w=["gfin"])
            wcnt = [0]
            pcnt = [0]
            gcn = [0]
            BK = lambda i: ("bk", i)
            pfin = [None]

            def slab(src, rkey, nk=16):
                wbuf = wcnt[0] % 2
                wcnt[0] += 1
                DMA("sp", wsl[wbuf][:, 0:nk, :], src, r=[rkey], w=[("wsl", wbuf)])
                return wsl[wbuf], ("wsl", wbuf)

            def load_pab(fg):
                wbuf = wcnt[0] % 2
                wcnt[0] += 1
                W = wsl[wbuf]
                wk = ("wsl", wbuf)
                DMA("sp", W[:, 0:8, :], wb_pa[:, :, fg * 512:(fg + 1) * 512], r=["wb_pa"], w=[wk])
                DMA("sp", W[:, 8:12, :], wb_pb[:, :, fg * 512:(fg + 1) * 512], r=["wb_pb"], w=[wk])
                return W, wk

            def rms4(dst_keys):
                for tt in range(4):
                    ACT(hb2[1], xr[:, tt, :], AF.Square, r=["xr"], w=[("hb2", 1), "ss"], accum=ssum[:, tt:tt + 1])
                ACT(ssum, ssum, AF.Ln, r=["ss"], w=["ss"], scale=1.0 / D, bias=1e-6)
                ACT(ssum, ssum, AF.Exp, r=["ss"], w=["ss"], scale=-0.5)

            for tg in range(J.T // 512):
                k0 = tg * 512
                DMA("sp", oa, J.oAT[:, :, k0:k0 + 512].rearrange("h e t -> e h t"), r=[("oAT", J.name)], w=["oa"])
                DMA("sp", obb, J.oBT[:, :, k0:k0 + 512].rearrange("h e t -> e h t"), r=[("oBT", J.name)], w=["obb"])
                nxt = load_pab(0)
                for hb_ in range(2):
                    B0 = 4 if hb_ == 0 else 0
                    sq4 = hb2[0].rearrange("p (a n) -> p a n", a=4)
                    o4 = oa[:, hb_ * 4:(hb_ + 1) * 4, :]
                    o4f = o4.rearrange("p a n -> p (a n)")
                    bks = [BK(B0 + i) for i in range(4)]
                    TT("dve", sq4, o4, o4, ALU.mult, r=["oa"], w=[("hb2", 0)])
                    for i in range(4):
                        MM(bank(B0 + i), ones, sq4[:, i, :], r=["ones", ("hb2", 0)], w=[BK(B0 + i)])
                    pr = ps[:, B0 * 512:(B0 + 4) * 512]
                    ACT(pr, pr, AF.Ln, r=bks, w=bks, scale=1.0 / 128, bias=1e-5)
                    ACT(pr, pr, AF.Exp, r=bks, w=bks, scale=-0.5)
                    STT("dve", o4f, o4f, gcol, pr, ALU.mult, ALU.mult, r=["oa", "gcol"] + bks, w=["oa"])
                if pfin[0] is not None:
                    pfin[0]()
                    pfin[0] = None
                for fg in range(4):
                    W, wk = nxt
                    if fg + 1 < 4:
                        nxt = load_pab(fg + 1)
                    if fg == 1:
                        DMA("sp", xr, J.x[OWN0 + k0:OWN0 + k0 + 512, :].rearrange("(t p) d -> p t d", p=128), w=["xr"])
                    for fc in range(4):
                        f = fg * 4 + fc
                        gb_ = gcn[0] % 2
                        gcn[0] += 1
                        DMA("sp", gsb[gb_][:, 0, :], J.gT[f, :, k0:k0 + 512], r=[("gT", J.name)], w=[("gsb", gb_)])
                        DMA("sp", gsb[gb_][:, 1, :], J.gT[16 + f, :, k0:k0 + 512], r=[("gT", J.name)], w=[("gsb", gb_)])
                        b0_ = 2 * (f % 2)
                        pa, pbk = bank(b0_), bank(b0_ + 1)
                        for h in range(8):
                            MM(pa, W[:, h, fc * 128:(fc + 1) * 128], oa[:, h, :], start=(h == 0), stop=(h == 7),
                               r=[wk, "oa"], w=[BK(b0_)])
                        for j in range(4):
                            MM(pbk, W[:, 8 + j, fc * 128:(fc + 1) * 128], obb[:, j, :], start=(j == 0), stop=(j == 3),
                               r=[wk, "obb"], w=[BK(b0_ + 1)])
                        TT("dve", tm1, pa, gsb[gb_][:, 0, :], ALU.mult, r=[BK(b0_), ("gsb", gb_)], w=["tm1"])
                        TT("dve", tm2, pbk, gsb[gb_][:, 1, :], ALU.mult, r=[BK(b0_ + 1), ("gsb", gb_)], w=["tm2"])
                        TT("pool", mh[:, f, :], tm1, tm2, ALU.add, r=["tm1", "tm2"], w=["mh", ("yo", 0), ("yo", 1)])
                for og in range(4):
                    W, wk = slab(wb_out[og], ("wb_out", og))
                    for tt in range(4):
                        pbn = pcnt[0] % 2
                        pcnt[0] += 1
                        P = bank(2 + pbn)
                        for k in range(16):
                            MM(P, mh[:, k, tt * 128:(tt + 1) * 128], W[:, k, :], start=(k == 0), stop=(k == 15),
                               r=["mh", wk], w=[BK(2 + pbn)])
                        xs_ = xr[:, tt, og * 512:(og + 1) * 512]
                        TT("dve", xs_, xs_, P, ALU.add, r=[BK(2 + pbn), "xr"], w=["xr"])
                rms4(None)
                for tt in range(4):
                    hbb = hb2[tt % 2]
                    hk = ("hb2", tt % 2)
                    TSC("dve", hbb, xr[:, tt, :], ssum[:, tt:tt + 1], None, ALU.mult, r=["xr", "ss"], w=[hk])
                    for half in range(2):
                        pb = bank(half).bitcast(BF16)
                        for c in range(8):
                            cc = half * 8 + c
                            TR(pb[:, c * 128:(c + 1) * 128], hbb[:, cc * 128:(cc + 1) * 128],
                               r=[hk, "ident"], w=[BK(half)])
                        TT("dve", mh[:, half * 8:(half + 1) * 8, tt * 128:(tt + 1) * 128],
                           pb.rearrange("p (c n) -> p c n", c=8), bcast_inner(gfc[:, half * 8:(half + 1) * 8], 128),
                           ALU.mult, r=[BK(half), "gfc"], w=["mh"])
                for fb in range(16):
                    W, wk = slab(wb_1[fb], ("wb_1", fb))
                    for fc in range(4):
                        pbn = pcnt[0] % 2
                        pcnt[0] += 1
                        P = bank(2 + pbn)
                        for k in range(16):
                            MM(P, W[:, k, fc * 128:(fc + 1) * 128], mh[:, k, :], start=(k == 0), stop=(k == 15),
                               r=["mh", wk], w=[BK(2 + pbn)])
                        tq = tm1 if pbn == 0 else tm2
                        tk = "tm1" if pbn == 0 else "tm2"
                        ACT(tq, P, AF.Square, r=[BK(2 + pbn)], w=[tk])
                        STT("dve", uT[:, fb * 4 + fc, :], P, 0.0, tq, ALU.is_gt, ALU.mult,
                            r=[BK(2 + pbn), tk], w=["uT"])
                for og in range(4):
                    for qd in range(4):
                        W, wk = slab(wb_2[og, qd], ("wb_2", og, qd))
                        for tt in range(4):
                            P = bank(4 + tt)
                            for k in range(16):
                                kk = qd * 16 + k
                                MM(P, uT[:, kk, tt * 128:(tt + 1) * 128], W[:, k, :], start=(kk == 0), stop=(kk == 63),
                                   r=["uT", wk], w=[BK(4 + tt)])
                    for tt in range(4):
                        xs_ = xr[:, tt, og * 512:(og + 1) * 512]
                        TT("dve", xs_, xs_, bank(4 + tt), ALU.add, r=[BK(4 + tt), "xr"], w=["xr"])
                def _fin(k0=k0):
                    rms4(None)
                    for tt in range(4):
                        a_ = tt % 2
                        STT("dve", yo[:, a_, :], xr[:, tt, :], ssum[:, tt:tt + 1], gfin, ALU.mult, ALU.mult,
                            r=["xr", "ss", "gfin"], w=["mh", ("yo", a_)])
                        DMA("act", J.y[k0 + tt * 128:k0 + (tt + 1) * 128, :], yo[:, a_, :], r=[("yo", a_)],
                            w=[("y", J.name)])
                pfin[0] = _fin
            if pfin[0] is not None:
                pfin[0]()
                pfin[0] = None
            AR.pop()
            S.barrier()

        for J in jobs:
            if 1 in phases:
                phase1(J)
        for J in jobs:
            if 2 in phases:
                phase2(J)
        for J in jobs:
            if 3 in phases:
                phase3(J)
        for J in jobs:
            if 4 in phases:
                phase4(J)
        stats = S.emit(nc, es)
    return nc, stats


def _rope_table(pos):
    pos = pos.astype(np.float32)
    out = np.zeros((pos.shape[0], 256), np.float32)
    for rot, off, nm in ((16, 0, 8), (32, 128, 4)):
        half = rot // 2
        inv = np.power(np.float32(500000.0), -np.arange(0, rot, 2, dtype=np.float32) / np.float32(rot)).astype(np.float32)
        ang = (pos[:, None] * inv[None, :]).astype(np.float32)
        out[:, off:off + 64] = np.tile(np.cos(ang), (1, nm))
        out[:, off + 64:off + 128] = np.tile(np.sin(ang), (1, nm))
    return out


def _vmask(valid_ext, nchunk):
    vm = np.zeros((128, nchunk, 3, 32), np.float32)
    i = np.arange(128)
    for ci in range(nchunk):
        cb = ci * 2048
        for g, r in enumerate(B_GROUPS_R):
            npr = 32 // r
            for rho in range(r):
                for jt in range(npr):
                    vm[:, ci, g, rho * npr + jt] = (valid_ext[cb + rho + r * (128 * jt + i)] - 1.0) * 30000.0
    return vm.reshape(128, nchunk * 3 * 32)


def _col_layout(g):
    return np.ascontiguousarray(np.asarray(g, np.float32).reshape(16, 128).T)


def make_in_maps(inputs, NSS, n_cores=8, jobs_enabled=("p", "s")):
    f32 = np.float32
    common = {
        "w_in": np.ascontiguousarray(inputs["w_in"][0], f32),
        "w_pa": np.ascontiguousarray(inputs["w_proj_a"][0], f32),
        "w_pb": np.ascontiguousarray(inputs["w_proj_b"][0], f32),
        "w_out": np.ascontiguousarray(inputs["w_out"][0], f32),
        "w_1": np.ascontiguousarray(inputs["w1"][0], f32),
        "w_2": np.ascontiguousarray(inputs["w2"][0], f32),
        "g_mix": _col_layout(inputs["norm_mix"][0]),
        "g_ffn": _col_layout(inputs["norm_ffn"][0]),
        "g_fin": np.ascontiguousarray(np.asarray(inputs["norm_final"], f32).reshape(1, D)),
        "lam_in": np.concatenate([np.asarray(inputs[k][0], f32) for k in
                                  ("lambda_q1", "lambda_k1", "lambda_q2", "lambda_k2")]).reshape(1, 256),
        "subln": np.ascontiguousarray(np.asarray(inputs["subln_g"][0], f32).reshape(128, 1)),
    }
    maps = []
    info = []
    per_seq = NSS // 4096
    for c in range(n_cores):
        m = dict(common)
        if "p" in jobs_enabled:
            xp = np.zeros((4096, D), f32)
            xp[1024:3072] = inputs["x_prompt"][c]
            pos = np.arange(4096) - 1024
            valid = ((pos >= 0) & (pos < 2048)).astype(f32)
            m["x_p"] = xp
            m["rope_p"] = _rope_table(np.clip(pos, 0, 2047))
            m["vmask_p"] = _vmask(valid, 1)
        if "s" in jobs_enabled:
            seq, c4 = c // per_seq, c % per_seq
            t0 = c4 * 4096
            pos_ext = t0 - 1024 + np.arange(6144)
            v_ext = (pos_ext >= 0) & (pos_ext < NSS)
            used = np.zeros(NSS, bool)
            used[pos_ext[v_ext]] = True
            far = np.nonzero(~used)[0]
            perm = np.empty(NSS, np.int64)
            ninv = int((~v_ext).sum())
            perm[:6144][v_ext] = pos_ext[v_ext]
            perm[:6144][~v_ext] = far[:ninv]
            perm[6144:] = far[ninv:]
            m["x_s"] = np.ascontiguousarray(np.asarray(inputs["x_sample"][seq], f32)[perm])
            m["rope_s"] = _rope_table(perm)
            m["vmask_s"] = _vmask(v_ext.astype(f32), 2)
            info.append((seq, t0))
        maps.append(m)
    return maps, info


_CACHE = {}


def kernel(**inputs):
    inputs = {k: np.asarray(v) for k, v in inputs.items()}
    NSS = inputs["x_sample"].shape[1]
    if NSS not in _CACHE:
        _CACHE[NSS] = build_program(NSS)[0]
    nc = _CACHE[NSS]
    maps, info = make_in_maps(inputs, NSS)
    res = run_bass_kernel_spmd(nc, maps, core_ids=list(range(8)))
    yp = np.stack([np.asarray(res.results[c]["y_p"], np.float32) for c in range(8)], axis=0)
    ys = np.zeros(inputs["x_sample"].shape, np.float32)
    for c, (seq, t0) in enumerate(info):
        ys[seq, t0:t0 + 4096] = np.asarray(res.results[c]["y_s"], np.float32)
    return (yp, ys)
```
